# Optimizing a Trainium2 kernel written in Bass

```python
import jax, jax.numpy as jnp
from jax import lax
import numpy as np

D_MODEL = 1024
BATCH = 4
SEQ = 8192
DEPTH = 1

EPS = 1e-6
N_MOD = 6
CONV_WIDTH = D_MODEL
CONV_KERNEL = 31
MLSTM_HEADS = 4
MLSTM_HEAD_DIM = D_MODEL // MLSTM_HEADS
MLSTM_WIDTH = MLSTM_HEADS * MLSTM_HEAD_DIM
QK_CONV = 4
CHUNK = 64
D_FF = ((8 * D_MODEL + 3 * 256 - 1) // (3 * 256)) * 256

N_GLU = 2 * CONV_WIDTH
N_QK = 2 * MLSTM_WIDTH
N_V = MLSTM_WIDTH
N_O = MLSTM_WIDTH
N_IF = 2 * MLSTM_HEADS
N_MERGE = 2 * D_MODEL
N_IN = N_GLU + N_QK + N_V + N_O + N_IF + N_MERGE
IN_SPLITS = (N_GLU, N_GLU + N_QK, N_GLU + N_QK + N_V, N_GLU + N_QK + N_V + N_O,
             N_GLU + N_QK + N_V + N_O + N_IF)
F_BIAS_OFFSET = N_GLU + N_QK + N_V + N_O + MLSTM_HEADS

kernel_name = "hybrid_conformer_mlstm_adaln_block"


def rms_norm(x, g):
    xf = x.astype(jnp.float32)
    y = xf * lax.rsqrt(jnp.mean(xf * xf, axis=-1, keepdims=True) + EPS)
    return (y * g.astype(jnp.float32)).astype(x.dtype)


def layer_norm(x, g, b):
    xf = x.astype(jnp.float32)
    mu = jnp.mean(xf, axis=-1, keepdims=True)
    var = jnp.mean(jnp.square(xf - mu), axis=-1, keepdims=True)
    y = (xf - mu) * lax.rsqrt(var + EPS)
    return (y * g.astype(jnp.float32) + b.astype(jnp.float32)).astype(x.dtype)


def causal_depthwise_conv(x, w, b):
    k = w.shape[0]
    y = lax.conv_general_dilated(
        x, w[:, None, :].astype(x.dtype), window_strides=(1,), padding=((k - 1, 0),),
        dimension_numbers=("NWC", "WIO", "NWC"), feature_group_count=x.shape[-1])
    return y + b


def mlstm_chunkwise(q, k, v, i_pre, f_pre):
    bsz, nh, s, dh = q.shape
    nc = s // CHUNK

    def to_chunks(t):
        return jnp.moveaxis(t.reshape(t.shape[:2] + (nc, CHUNK) + t.shape[3:]), 2, 0)

    log_f = jax.nn.log_sigmoid(f_pre)
    xs = (to_chunks(q), to_chunks(k), to_chunks(v), to_chunks(i_pre), to_chunks(log_f))
    causal = jnp.tril(jnp.ones((CHUNK, CHUNK), dtype=bool))

    def step(carry, inp):
        c_st, n_st, m_st = carry
        qc, kc, vc, ic, lfc = inp
        b = jnp.cumsum(lfc, axis=-1)
        d_log = jnp.where(causal, b[..., :, None] - b[..., None, :] + ic[..., None, :], -jnp.inf)
        inter_log = b + m_st[..., None]
        m_t = jnp.maximum(inter_log, jnp.max(d_log, axis=-1))
        dmat = jnp.exp(d_log - m_t[..., None])
        inter_w = jnp.exp(inter_log - m_t)
        sc = jnp.einsum("bhtd,bhsd->bhts", qc, kc) * dmat
        num = (jnp.einsum("bhts,bhse->bhte", sc, vc)
               + inter_w[..., None] * jnp.einsum("bhtd,bhde->bhte", qc, c_st))
        qn = jnp.sum(sc, axis=-1) + inter_w * jnp.einsum("bhtd,bhd->bht", qc, n_st)
        h = num / jnp.maximum(jnp.abs(qn), jnp.exp(-m_t))[..., None]
        b_last = b[..., -1]
        a = b_last[..., None] - b + ic
        m_new = jnp.maximum(b_last + m_st, jnp.max(a, axis=-1))
        decay = jnp.exp(b_last + m_st - m_new)
        w = jnp.exp(a - m_new[..., None])
        c_new = decay[..., None, None] * c_st + jnp.einsum("bhs,bhsd,bhse->bhde", w, kc, vc)
        n_new = decay[..., None] * n_st + jnp.einsum("bhs,bhsd->bhd", w, kc)
        return (c_new, n_new, m_new), h

    init = (jnp.zeros((bsz, nh, dh, dh), jnp.float32),
            jnp.zeros((bsz, nh, dh), jnp.float32),
            jnp.zeros((bsz, nh), jnp.float32))
    _, h = lax.scan(step, init, xs)
    return jnp.moveaxis(h, 0, 2).reshape(bsz, nh, s, dh)


def setup_inputs(seed: int = 0) -> dict:
    key = jax.random.key(seed)
    ks = jax.random.split(key, 24)
    f32 = jnp.float32

    def nrm(k, shape, scale):
        return jax.random.normal(k, shape, f32) * scale

    x = jax.random.normal(ks[0], (BATCH, SEQ, D_MODEL), f32)
    c = jax.random.normal(ks[1], (BATCH, D_MODEL), f32)
    w_ada = nrm(ks[2], (DEPTH, D_MODEL, N_MOD * D_MODEL), 0.5 * D_MODEL ** -0.5)
    b_ada = nrm(ks[3], (DEPTH, N_MOD * D_MODEL), 0.02)
    norm_mix_g = 1.0 + nrm(ks[4], (DEPTH, D_MODEL), 0.1)
    w_in = nrm(ks[5], (DEPTH, D_MODEL, N_IN), D_MODEL ** -0.5)
    f_bias = jnp.linspace(3.0, 6.0, MLSTM_HEADS, dtype=f32)
    b_in = nrm(ks[6], (DEPTH, N_IN), 0.02)
    b_in = b_in.at[:, F_BIAS_OFFSET:F_BIAS_OFFSET + MLSTM_HEADS].add(f_bias)
    conv_dw_w = nrm(ks[7], (DEPTH, CONV_KERNEL, CONV_WIDTH), CONV_KERNEL ** -0.5)
    conv_dw_b = nrm(ks[8], (DEPTH, CONV_WIDTH), 0.02)
    conv_ln_g = 1.0 + nrm(ks[9], (DEPTH, CONV_WIDTH), 0.1)
    conv_ln_b = nrm(ks[10], (DEPTH, CONV_WIDTH), 0.02)
    w_conv_pw = nrm(ks[11], (DEPTH, CONV_WIDTH, D_MODEL), CONV_WIDTH ** -0.5)
    b_conv_pw = nrm(ks[12], (DEPTH, D_MODEL), 0.02)
    qk_conv_w = nrm(ks[13], (DEPTH, QK_CONV, N_QK), QK_CONV ** -0.5)
    qk_conv_b = nrm(ks[14], (DEPTH, N_QK), 0.02)
    mlstm_norm_g = 1.0 + nrm(ks[15], (DEPTH, MLSTM_WIDTH), 0.1)
    w_out = nrm(ks[16], (DEPTH, D_MODEL, D_MODEL), D_MODEL ** -0.5)
    norm_ffn_g = 1.0 + nrm(ks[17], (DEPTH, D_MODEL), 0.1)
    w_ffn_up = nrm(ks[18], (DEPTH, D_MODEL, 2 * D_FF), D_MODEL ** -0.5)
    w_ffn_down = nrm(ks[19], (DEPTH, D_FF, D_MODEL), D_FF ** -0.5)
    final_norm_g = 1.0 + nrm(ks[20], (D_MODEL,), 0.1)
    return {"x": x, "c": c, "w_ada": w_ada, "b_ada": b_ada, "norm_mix_g": norm_mix_g,
            "w_in": w_in, "b_in": b_in, "conv_dw_w": conv_dw_w, "conv_dw_b": conv_dw_b,
            "conv_ln_g": conv_ln_g, "conv_ln_b": conv_ln_b, "w_conv_pw": w_conv_pw,
            "b_conv_pw": b_conv_pw, "qk_conv_w": qk_conv_w, "qk_conv_b": qk_conv_b,
            "mlstm_norm_g": mlstm_norm_g, "w_out": w_out, "norm_ffn_g": norm_ffn_g,
            "w_ffn_up": w_ffn_up, "w_ffn_down": w_ffn_down, "final_norm_g": final_norm_g}


def reference(x, c, w_ada, b_ada, norm_mix_g, w_in, b_in, conv_dw_w, conv_dw_b,
              conv_ln_g, conv_ln_b, w_conv_pw, b_conv_pw, qk_conv_w, qk_conv_b,
              mlstm_norm_g, w_out, norm_ffn_g, w_ffn_up, w_ffn_down, final_norm_g):
    bsz, s, _ = x.shape
    for l in range(DEPTH):
        mod = jnp.einsum("bd,de->be", jax.nn.silu(c), w_ada[l]) + b_ada[l]
        sh1, sc1, g1, sh2, sc2, g2 = jnp.split(mod[:, None, :], N_MOD, axis=-1)

        h = rms_norm(x, norm_mix_g[l]) * (1.0 + sc1) + sh1
        proj = jnp.einsum("bsd,de->bse", h, w_in[l]) + b_in[l]
        glu_in, qk_in, v, o_pre, if_pre, merge_pre = jnp.split(proj, IN_SPLITS, axis=-1)

        a = glu_in[..., :CONV_WIDTH] * jax.nn.sigmoid(glu_in[..., CONV_WIDTH:])
        a = causal_depthwise_conv(a, conv_dw_w[l], conv_dw_b[l])
        a = jax.nn.silu(layer_norm(a, conv_ln_g[l], conv_ln_b[l]))
        y_conv = jnp.einsum("bsc,cd->bsd", a, w_conv_pw[l]) + b_conv_pw[l]

        qk = jax.nn.silu(causal_depthwise_conv(qk_in, qk_conv_w[l], qk_conv_b[l]))
        q, k = jnp.split(qk, 2, axis=-1)

        def heads(t):
            return t.reshape(bsz, s, MLSTM_HEADS, MLSTM_HEAD_DIM).transpose(0, 2, 1, 3).astype(jnp.float32)

        gates = jnp.moveaxis(if_pre, -1, 1).astype(jnp.float32)
        hm = mlstm_chunkwise(heads(q), heads(k) * (MLSTM_HEAD_DIM ** -0.5), heads(v),
                             gates[:, :MLSTM_HEADS], gates[:, MLSTM_HEADS:])
        mu = jnp.mean(hm, axis=-1, keepdims=True)
        var = jnp.mean(jnp.square(hm - mu), axis=-1, keepdims=True)
        hm = ((hm - mu) * lax.rsqrt(var + EPS)).transpose(0, 2, 1, 3).reshape(bsz, s, MLSTM_WIDTH)
        hm = (hm * mlstm_norm_g[l].astype(jnp.float32)).astype(x.dtype)
        y_mlstm = jax.nn.sigmoid(o_pre) * hm

        g_conv, g_mlstm = jnp.split(jax.nn.sigmoid(merge_pre), 2, axis=-1)
        y = jnp.einsum("bsd,de->bse", g_conv * y_conv + g_mlstm * y_mlstm, w_out[l])
        x = x + g1 * y

        h = rms_norm(x, norm_ffn_g[l]) * (1.0 + sc2) + sh2
        gt, up = jnp.split(jnp.einsum("bsd,df->bsf", h, w_ffn_up[l]), 2, axis=-1)
        x = x + g2 * jnp.einsum("bsf,fd->bsd", jax.nn.silu(gt) * up, w_ffn_down[l])
    return rms_norm(x, final_norm_g)
```

```python
import math
from contextlib import ExitStack

import numpy as np
import concourse.bass as bass
import concourse.mybir as mybir
from concourse.bass_utils import run_bass_kernel_spmd

F32 = mybir.dt.float32
BF16 = mybir.dt.bfloat16
AF = mybir.ActivationFunctionType
ALU = mybir.AluOpType
AX = mybir.AxisListType

D = 1024
NIN = 8200
DFF = 2816
T = 512
EPS = 1e-6
LN16 = math.log(16.0)


class Op:
    __slots__ = ("eng", "fn", "r", "w", "dma", "deps", "sig", "cnt", "dsem", "dval", "dprev", "out")

    def __init__(self, eng, fn, r, w, dma, out):
        self.eng = eng
        self.fn = fn
        self.r = tuple(r)
        self.w = tuple(w)
        self.dma = dma
        self.out = out
        self.deps = {}
        self.sig = False
        self.cnt = 0
        self.dsem = None
        self.dval = 0
        self.dprev = 0


class _Rec:
    def __getattr__(self, name):
        return lambda *a, **k: (name, a, k)


_REC = _Rec()


class Sched:
    ENGS = ("pe", "act", "dve", "pool", "sp")

    def __init__(self, n_dma_sems):
        self.ops = []
        self.n_dma_sems = n_dma_sems

    def add(self, eng, fn, r=(), w=(), dma=False, out=False):
        self.ops.append(Op(eng, fn(_REC), r, w, dma, out))

    def pe(self, fn, r=(), w=()):
        self.add("pe", fn, r, w)

    def act(self, fn, r=(), w=()):
        self.add("act", fn, r, w)

    def dve(self, fn, r=(), w=()):
        self.add("dve", fn, r, w)

    def pool(self, fn, r=(), w=()):
        self.add("pool", fn, r, w)

    def dma(self, q, fn, r=(), w=(), out=False):
        self.add(q, fn, r, w, dma=True, out=out)

    def analyze(self):
        ops = self.ops
        last_w = {}
        readers = {}
        for i, op in enumerate(ops):
            deps = {}
            for k in op.r:
                if k in last_w:
                    deps[last_w[k]] = True
            for k in op.w:
                if k in last_w:
                    deps[last_w[k]] = True
                for j in readers.get(k, ()):
                    deps.setdefault(j, False)
            for k in op.r:
                readers.setdefault(k, []).append(i)
            for k in op.w:
                last_w[k] = i
                readers[k] = []
            deps.pop(i, None)
            need = {}
            for d, raw in deps.items():
                dop = ops[d]
                if dop.dma:
                    need[d] = True
                elif dop.eng == op.eng and not op.dma:
                    if op.eng != "pe":
                        need[d] = True
                else:
                    need[d] = True
            op.deps = need
            for d in need:
                ops[d].sig = True
        cnt = {e: 0 for e in self.ENGS}
        dma_i = {e: 0 for e in self.ENGS}
        dma_last = {}
        for op in ops:
            if op.dma:
                k = (op.eng, dma_i[op.eng] % self.n_dma_sems[op.eng])
                dma_i[op.eng] += 1
                op.dsem = k
                op.dprev = dma_last.get(k, 0)
                op.dval = op.dprev + 16
                dma_last[k] = op.dval
            elif op.sig:
                cnt[op.eng] += 1
                op.cnt = cnt[op.eng]

    def emit(self, block, sems, dsems):
        ops = self.ops
        self.analyze()
        out_waits = {}
        for op in ops:
            if op.dma and op.out:
                out_waits[op.dsem] = max(out_waits.get(op.dsem, 0), op.dval)

        def run_engine(ename, eng):
            waited = {}

            def wait(sem_key, sem, val):
                if waited.get(sem_key, 0) >= val:
                    return
                waited[sem_key] = val
                eng.wait_ge(sem, val)

            for op in ops:
                if op.eng != ename:
                    continue
                for d in sorted(op.deps):
                    dop = ops[d]
                    if dop.dma:
                        wait(dop.dsem, dsems[dop.dsem], dop.dval)
                    else:
                        wait(dop.eng, sems[dop.eng], dop.cnt)
                if op.dma:
                    if op.dprev > 0:
                        wait(op.dsem, dsems[op.dsem], op.dprev)
                    name, a, k = op.fn
                    getattr(eng, name)(*a, **k).then_inc(dsems[op.dsem], 16)
                else:
                    name, a, k = op.fn
                    ins = getattr(eng, name)(*a, **k)
                    if op.sig:
                        ins.then_inc(sems[ename], 1)
            if ename == "sp":
                for k, v in out_waits.items():
                    wait(k, dsems[k], v)

        block.tensor(lambda e: run_engine("pe", e))
        block.scalar(lambda e: run_engine("act", e))
        block.vector(lambda e: run_engine("dve", e))
        block.gpsimd(lambda e: run_engine("pool", e))
        block.sync(lambda e: run_engine("sp", e))


_COLS = {}
_off = 0
for _n, _w in [("b_glu", 16), ("b_qk", 16), ("b_o", 8), ("b_mc", 8), ("b_mm", 8), ("convw", 248), ("convb", 8),
               ("lng", 8), ("lnb", 8), ("b_pw", 8), ("qkw", 64), ("qkb", 16), ("mng", 8), ("nmg", 8), ("nfg", 8),
               ("bada", 32), ("ccol", 8), ("flag", 1), ("b_i", 1), ("b_f", 1)]:
    _COLS[_n] = _off
    _off += _w
NCOLS = _off


def build(NS, NP):
    nc = bass.Bass("TRN2", target_bir_lowering=False)
    NM, NPT = NS * T, NP * T

    def din(name, shape):
        return nc.dram_tensor(name, shape, F32, kind="ExternalInput").ap()

    xm = din("xm", [NM, D])
    xp = din("xp", [NPT, D])
    cols_d = din("cols", [128, NCOLS])
    rows_d = din("rows", [4, D])
    w_ada = din("w_ada", [D, 6 * D])
    w_in = din("w_in", [D, NIN])
    w_pw = din("w_pw", [D, D])
    w_out = din("w_out", [D, D])
    w_up = din("w_up", [D, 2 * DFF])
    w_dn = din("w_dn", [DFF, D])
    out_d = nc.dram_tensor("out", [NM, D], F32, kind="ExternalOutput").ap()
    s_in = nc.dram_tensor("s_in", [D, NIN], BF16).ap()
    s_pw = nc.dram_tensor("s_pw", [D, D], BF16).ap()
    s_out = nc.dram_tensor("s_out", [D, D], BF16).ap()
    s_up = nc.dram_tensor("s_up", [D, 2 * DFF], BF16).ap()
    s_dn = nc.dram_tensor("s_dn", [DFF, D], BF16).ap()
    s_dg = nc.dram_tensor("s_dg", [16, 128, 16 * 128], BF16).ap()
    s_dq = nc.dram_tensor("s_dq", [8, 128, 1024], BF16).ap()

    NDS = {"sp": 16, "pool": 8, "act": 1, "pe": 1, "dve": 1}
    S = Sched(NDS)

    with ExitStack() as es:
        def sb(name, shape, dt=F32):
            return es.enter_context(nc.sbuf_tensor(name, shape, dt))

        sems = {e: es.enter_context(nc.semaphore(f"s_{e}")) for e in Sched.ENGS}
        dsems = {}
        for e in ("sp", "pool"):
            for i in range(NDS[e]):
                dsems[(e, i)] = es.enter_context(nc.semaphore(f"d_{e}{i}"))

        banks = [es.enter_context(nc.psum_tensor(f"bank{i}", [128, 512], F32)) for i in range(8)]
        bank_i = [0]

        held = set()

        def nb(hold=False):
            for _ in range(9):
                i = bank_i[0]
                bank_i[0] = (i + 1) % 8
                if i not in held:
                    break
            else:
                raise RuntimeError("no free PSUM bank")
            if hold:
                held.add(i)
            return banks[i], ("bank", i)

        def rel(*keys):
            for k in keys:
                held.discard(k[1])

        cols = sb("cols_sb", [128, NCOLS])
        ident = sb("ident", [128, 128])
        identb = sb("identb", [128, 128], BF16)
        onesf = sb("onesf", [128, 128])
        ones1 = sb("ones1", [128, 128])
        onesb = sb("onesb", [128, 128], BF16)
        maskT = sb("maskT", [128, 128])
        Gf = sb("Gf", [128, D])
        bvb = sb("bvb", [1, D], BF16)
        modc = sb("modc", [128, 32])
        G1c = sb("G1c", [128, 8])
        G2c = sb("G2c", [128, 8])
        scc = sb("scc", [128, 8])
        modr = sb("modr", [128, 32])
        nbf_ = sb("nbf_", [4, 1])
        wg = sb("wg", [128, 8, 8], BF16)
        xts = [sb(f"xt{i}", [128, 4, D]) for i in range(2)]
        xsb = [sb(f"xsb{i}", [128, D], BF16) for i in range(2)]
        junk = xsb[1]
        ptmp = sb("ptmp", [128, T])
        ssqs = [sb(f"ssq{i}", [128, 4]) for i in range(3)]
        rstds = [sb(f"rstd{i}", [128, 4]) for i in range(3)]
        sqj = sb("sqj", [128, D], BF16)
        dg = [sb(f"dg{i}", [128, 128]) for i in range(4)]
        hT = sb("hT", [128, 8, T], BF16)
        aT = sb("aT", [128, 8, T + 30], BF16)
        big = sb("big", [128, 6144])
        bigb = big[:].bitcast(BF16)
        G1g = big[:, 0:D]
        G2g = big[:, D:2 * D]
        G1K = [("big", i) for i in range(0, 4)]
        G2K = [("big", i) for i in range(4, 8)]
        qkb16 = [sb(f"qkb16_{i}", [128, T + 4], BF16) for i in range(3)]
        halo = sb("halo", [128, 48], BF16)
        dgbuf = [sb(f"dgbuf{i}", [128, 16 * 128], BF16) for i in range(2)]
        dqbuf = [sb(f"dqbuf{i}", [128, 1024], BF16) for i in range(2)]
        qkT = sb("qkT", [128, 16, T], BF16)
        ktok = [sb(f"ktok{i}", [128, D], BF16) for i in range(2)]
        vtok = sb("vtok", [128, 4, D], BF16)
        tmpf = [sb(f"tmpf{i}", [128, T]) for i in range(5)]
        Cst = sb("Cst", [128, 4, 512])
        Cbf = sb("Cbf", [128, 4, 512], BF16)
        nst = sb("nst", [128, 8])
        nbf = sb("nbf", [128, 8], BF16)
        mst = sb("mst", [4, 1])
        hmtok = [sb(f"hmtok{i}", [128, D], BF16) for i in range(2)]
        hmT = sb("hmT", [128, 8, T], BF16)
        scB = hmtok[0][:].rearrange("p (k n) -> p k n", k=8)
        wring = [sb(f"wr{i}", [128, 2048], BF16) for i in range(6)]
        g_i = sb("g_i", [4, T])
        g_lf = sb("g_lf", [4, T])
        g_nb = sb("g_nb", [4, T])
        g_t, g_w, g_d = g_lf, g_i, g_nb
        g_one = sb("g_one", [4, 128])
        g_um = sb("g_um", [4, 4])
        g_M = sb("g_M", [4, 4])
        g_dm = sb("g_dm", [4, 4])
        g_io = sb("g_io", [4, 4])
        g_iod = sb("g_iod", [4, 16])
        gcol = sb("gcol", [128, 32])
        iob = sb("iob", [128, 16])
        st6 = sb("st6", [128, 4, 6])
        mv = sb("mv", [128, 4, 2])
        sm = sb("sm", [128, 32])
        mean_sb = sb("mean_sb", [128, T])
        rstd_sb = sb("rstd_sb", [128, T])

        block = es.enter_context(nc.Block())

        def C(name, i=0, n=1):
            o = _COLS[name] + i
            return cols[:, o:o + n]

        S.dma("sp", lambda e: e.dma_start(out=cols[:], in_=cols_d), w=["cols"])
        for i, (tile_, keys_) in enumerate([(G1g, G1K), (G2g, G2K), (Gf[:], ["Gf"])]):
            S.dma("sp", lambda e: e.dma_start(out=tile_, in_=rows_d[i:i + 1, :].partition_broadcast(128)), w=keys_)
        S.dma("pool", lambda e: e.dma_start(out=bvb[:], in_=rows_d[3:4, :]), w=["bvb"])

        scr_keys = {}
        in_pieces = [(3072, 5120), (6144, 6152), (2048, 3072), (0, 2048), (5120, 6144), (6152, 8200)]

        def in_keys(c0, c1):
            ks = []
            for (a, b) in in_pieces:
                if a < c1 and c0 < b:
                    ks += [("scr", "in", a, rb) for rb in range(0, 8, 2)]
            return ks

        def in_piece_task(a, b, rb):
            return (w_in[rb * 128:(rb + 2) * 128, a:b], s_in[rb * 128:(rb + 2) * 128, a:b], ("scr", "in", a, rb))

        for (a, b) in in_pieces[:2]:
            for rb in range(0, 8, 2):
                src_, dst_, k_ = in_piece_task(a, b, rb)
                S.dma("pool", lambda e: e.dma_start(out=dst_, in_=src_), w=[k_])
        rest_list = []
        for (a, b) in in_pieces[2:]:
            for rb in range(0, 8, 2):
                rest_list.append(in_piece_task(a, b, rb))
        up_list = []
        for name, src, dst, rows_ in (("pw", w_pw, s_pw, D), ("up", w_up, s_up, D)):
            scr_keys[name] = [("scr", name, rb) for rb in range(rows_ // 128)]
            for rb in range(rows_ // 128):
                (up_list if name == "up" else rest_list).append((src[rb * 128:(rb + 1) * 128, :], dst[rb * 128:(rb + 1) * 128, :], ("scr", name, rb)))
        scr_keys["out"] = [("scr", "out", rb) for rb in range(8)]
        scr_keys["dn"] = [("scr", "dn", rb) for rb in range(22)]

        scale_tasks = [(name, src, dst, rb, Gt, gk) for name, src, dst, nrb, Gt, gk in
                       (("out", w_out, s_out, 8, G1g, G1K), ("dn", w_dn, s_dn, 22, G2g, G2K)) for rb in range(nrb)]
        diag_tasks = list(range(8))
        stg_i = [0]

        def scaled_scratch(n):
            for _ in range(n):
                if not scale_tasks:
                    break
                name, src, dst, rb, Gt, gk = scale_tasks.pop(0)
                if True:
                    for ch in range(2):
                        si = stg_i[0]
                        stg_i[0] = (si + 1) % 2
                        a32, ak = (mean_sb, "mean_sb") if si == 0 else (rstd_sb, "rstd_sb")
                        ob, ok_ = hmtok[1 - si][:, 0:512], ("hmtok", 1 - si)
                        S.dma("pool", lambda e: e.dma_start(out=a32[:], in_=src[rb * 128:(rb + 1) * 128, ch * 512:(ch + 1) * 512]), w=[ak])
                        S.pool(lambda e: e.tensor_tensor(out=ob, in0=a32[:], in1=Gt[:, ch * 512:(ch + 1) * 512], op=ALU.mult), r=[ak] + gk, w=[ok_])
                        S.dma("pool", lambda e: e.dma_start(out=dst[rb * 128:(rb + 1) * 128, ch * 512:(ch + 1) * 512], in_=ob), r=[ok_], w=[("scr", name, rb, ch)])
        gate_t = sb("gate_t", [1, 4])

        def cast_rest(st, n, xb):
            S.dve(lambda e: e.memset(gate_t[:, 0:1], 0.0), r=[("x", xb, 0)], w=[("gate", st)])
            for _ in range(n):
                if not rest_list:
                    return
                src_, dst_, k = rest_list.pop(0)
                S.dma("pool", lambda e: e.dma_start(out=dst_, in_=src_), r=[("gate", st)], w=[k])

        S.pool(lambda e: e.memset(ident[:], 0.0), w=["ident"])
        S.pool(lambda e: e.affine_select(out=ident[:], in_=ident[:], compare_op=ALU.not_equal, fill=1.0, base=0, pattern=[[-1, 128]], channel_multiplier=1), r=["ident"], w=["ident"])
        S.pool(lambda e: e.memset(maskT[:], 1.0), w=["maskT"])
        S.pool(lambda e: e.affine_select(out=maskT[:], in_=maskT[:], compare_op=ALU.is_ge, fill=0.0, base=0, pattern=[[1, 128]], channel_multiplier=-1), r=["maskT"], w=["maskT"])
        S.pool(lambda e: e.memset(onesf[:], 1.0 / D), w=["onesf"])
        S.pool(lambda e: e.memset(ones1[:], 1.0), w=["ones1"])
        S.pool(lambda e: e.memset(onesb[:], 1.0), w=["onesb"])
        S.pool(lambda e: e.memset(g_one[:], 1.0), w=["g_one"])
        S.pool(lambda e: e.memset(Cst[:], 0.0), w=[("C", h) for h in range(4)])
        S.pool(lambda e: e.memset(nst[:], 0.0), w=["nst"])
        S.pool(lambda e: e.memset(mst[:], 0.0), w=["mst"])
        S.pool(lambda e: e.memset(halo[:], 0.0), w=["halo"])
        S.pool(lambda e: e.memset(aT[:], 0.0), w=[("aT", cc) for cc in range(8)])
        S.dve(lambda e: e.tensor_copy(out=identb[:], in_=ident[:]), r=["ident"], w=["identb"])
        S.dve(lambda e: e.tensor_scalar(out=nbf_[:], in0=cols[0:4, _COLS["b_f"]:_COLS["b_f"] + 1], scalar1=-1.0, scalar2=None, op0=ALU.mult), r=["cols"], w=["nbf_"])

        tf_i = [0]

        def tmp():
            i = tf_i[0]
            tf_i[0] = (i + 1) % len(tmpf)
            return tmpf[i], ("tmpf", i)

        S.act(lambda e: e.activation(out=scc[:], in_=C("ccol", 0, 8), func=AF.Silu), r=["cols"], w=["scc"])
        for kc in range(8):
            S.dve(lambda e, kc=kc: e.tensor_scalar(out=scB[:, kc, :], in0=ones1[:], scalar1=scc[:, kc:kc + 1], scalar2=None, op0=ALU.mult), r=["ones1", "scc"], w=[("hmtok", 0)])
        wr_i = [0]

        def wslot():
            i = wr_i[0]
            wr_i[0] = (i + 1) % len(wring)
            return wring[i], ("wr", i)

        col_forms = {0: 0, 1: 1, 3: 2, 4: 3}

        def mod_vec(vi):
            for blk in range(4):
                c0 = vi * D + blk * 256
                slot, sk = wslot()
                wv = slot[:].rearrange("p (k n) -> p k n", k=8)
                S.dma("pool", lambda e: e.dma_start(out=wv, in_=w_ada[:, c0:c0 + 256].rearrange("(k p) n -> p k n", p=128)), w=[sk])
                br, kr = nb()
                for kc in range(8):
                    S.pe(lambda e: e.matmul(br[:, 0:256], lhsT=scB[:, kc, :], rhs=wv[:, kc, :], start=(kc == 0), stop=(kc == 7)), r=[sk, ("hmtok", 0)], w=[kr])
                if vi in col_forms:
                    mk = "modr1" if vi < 2 else "modr2"
                    for sub in range(2):
                        cidx = col_forms[vi] * 8 + blk * 2 + sub
                        t_, tk = tmp()
                        S.dve(lambda e: e.tensor_tensor(out=t_[:, 0:128], in0=br[:, sub * 128:(sub + 1) * 128], in1=ident[:], op=ALU.mult), r=["ident"], w=[tk, kr])
                        S.dve(lambda e: e.tensor_reduce(out=modr[:, cidx:cidx + 1], in_=t_[:, 0:128], axis=AX.X, op=ALU.add), r=[tk], w=[mk])
                else:
                    Gt, gk = (G1g, G1K) if vi == 2 else (G2g, G2K)
                    S.dve(lambda e: e.tensor_tensor(out=Gt[:, blk * 256:(blk + 1) * 256], in0=br[:, 0:256], in1=Gt[:, blk * 256:(blk + 1) * 256], op=ALU.add), r=gk, w=gk + [kr])

        def mod_part1():
            mod_vec(1)
            mod_vec(0)
            S.dve(lambda e: e.tensor_tensor(out=modc[:, 0:16], in0=modr[:, 0:16], in1=C("bada", 0, 16), op=ALU.add), r=["cols", "modr1"], w=["modc1"])
            S.dve(lambda e: e.scalar_tensor_tensor(out=G1c[:], in0=modc[:, 8:16], scalar=1.0, in1=C("nmg", 0, 8), op0=ALU.add, op1=ALU.mult), r=["modc1", "cols"], w=["G1c"])

        def mod_part2():
            mod_vec(2)
            mod_vec(4)
            mod_vec(3)
            S.dve(lambda e: e.tensor_tensor(out=modc[:, 16:32], in0=modr[:, 16:32], in1=C("bada", 16, 16), op=ALU.add), r=["cols", "modr2"], w=["modc2"])
            S.dve(lambda e: e.scalar_tensor_tensor(out=G2c[:], in0=modc[:, 24:32], scalar=1.0, in1=C("nfg", 0, 8), op0=ALU.add, op1=ALU.mult), r=["modc2", "cols"], w=["G2c"])
            mod_vec(5)


        def wload(src, keys, r0, nr, c0, ncol):
            slot, sk = wslot()
            if keys is None:
                keys = in_keys(c0, c0 + ncol)
            v = slot[:, 0:nr * ncol].rearrange("p (k n) -> p k n", k=nr)
            S.dma("sp", lambda e: e.dma_start(out=v, in_=src[r0:r0 + nr * 128, c0:c0 + ncol].rearrange("(k p) n -> p k n", p=128)), r=keys, w=[sk])
            return v, sk

        stats_done = {}

        def rms_stats(xb, si):
            xt = xts[xb]
            ssq_, sk_ = ssqs[si], ("ssq", si)
            rstd_, rk_ = rstds[si], ("rstd", si)
            for j in range(4):
                S.act(lambda e: e.activation(out=sqj[:], in_=xt[:, j, :], func=AF.Square, accum_out=ssq_[:, j:j + 1]), r=[("x", xb, j)], w=["sqj", sk_])
            S.act(lambda e: e.activation(out=rstd_[:], in_=ssq_[:], func=AF.Sqrt, scale=1.0 / D, bias=EPS), r=[sk_], w=[rk_])
            S.dve(lambda e: e.reciprocal(out=rstd_[:], in_=rstd_[:]), r=[rk_], w=[rk_])
            if si == 0:
                stats_done[xb] = True

        def rmsnorm_T(Gc, shc_off, xb, dst=None):
            xt = xts[xb]
            if dst is None:
                dst, dkey = hT, (lambda fc: [("hT", fc)])
            else:
                dkey = lambda fc: [("hmT", j) for j in range(4)]
            mkey, gkey = ("modc1", "G1c") if shc_off == 0 else ("modc2", "G2c")
            si = 0 if shc_off == 0 else 1
            rstd, rk = rstds[si], ("rstd", si)
            if not (si == 0 and stats_done.get(xb)):
                rms_stats(xb, si)
            stats_done[xb] = False
            bks = [nb(hold=True) for _ in range(4)]
            for j in range(4):
                xs, xsk = xsb[j % 2], ("xs", j % 2)
                S.act(lambda e: e.activation(out=xs[:], in_=xt[:, j, :], func=AF.Copy, scale=rstd[:, j:j + 1]), r=[("x", xb, j), rk], w=[xsk])
                for fc in range(8):
                    b, bk = bks[fc // 2]
                    col = ((fc % 2) * 4 + j) * 128
                    S.pe(lambda e: e.transpose(out=b[:].bitcast(BF16)[:, col:col + 128], in_=xs[:, fc * 128:(fc + 1) * 128], identity=identb[:]), r=[xsk, "identb"], w=[bk])
            for fc in range(8):
                b, bk = bks[fc // 2]
                src = b[:].bitcast(BF16)[:, (fc % 2) * 512:(fc % 2 + 1) * 512]
                if (fc // 2) % 2 == 0:
                    S.act(lambda e: e.activation(out=dst[:, fc, :], in_=src, func=AF.Identity, scale=Gc[:, fc:fc + 1], bias=modc[:, shc_off + fc:shc_off + fc + 1]), r=[mkey, gkey], w=dkey(fc) + [bk])
                else:
                    S.dve(lambda e: e.tensor_scalar(out=dst[:, fc, :], in0=src, scalar1=Gc[:, fc:fc + 1], scalar2=modc[:, shc_off + fc:shc_off + fc + 1], op0=ALU.mult, op1=ALU.add), r=[mkey, gkey], w=dkey(fc) + [bk])
            rel(*[k for _, k in bks])

        hT_all = [("hT", k) for k in range(8)]

        def projA(wv, wk, ml, rhs_keys=hT_all, rhs=None):
            b, bk = nb()
            for kc in range(8):
                S.pe(lambda e, kc=kc: e.matmul(b[:], lhsT=wv[:, kc, ml * 128:(ml + 1) * 128], rhs=(hT if rhs is None else rhs)[:, kc, :], start=(kc == 0), stop=(kc == 7)), r=[wk] + list(rhs_keys), w=[bk])
            return b, bk

        IN_K = None

        dg_i = [0]
        dq_i = [0]
        qb_i = [0]

        def dgload(cc, hf):
            i = dg_i[0]
            dg_i[0] = (i + 1) % 2
            S.dma("sp", lambda e: e.dma_start(out=dgbuf[i][:], in_=s_dg[2 * cc + hf]), r=[("sdg", 2 * cc + hf)], w=[("dgbuf", i)])
            return dgbuf[i], ("dgbuf", i)

        def dqload(p):
            i = dq_i[0]
            dq_i[0] = (i + 1) % 2
            S.dma("sp", lambda e: e.dma_start(out=dqbuf[i][:], in_=s_dq[p]), r=[("sdq", p)], w=[("dqbuf", i)])
            return dqbuf[i], ("dqbuf", i)

        def make_diag_scratch_qk():
            for p in range(8):
                i = dq_i[0]
                dq_i[0] = (i + 1) % 2
                for ml in range(2):
                    for j in range(4):
                        o = (ml * 4 + j) * 128
                        S.dve(lambda e: e.tensor_scalar(out=dqbuf[i][:, o:o + 128], in0=identb[:], scalar1=C("qkw", (2 * p + ml) * 4 + j), scalar2=None, op0=ALU.mult), r=["identb", "cols"], w=[("dqbuf", i)])
                S.dma("sp", lambda e: e.dma_start(out=s_dq[p], in_=dqbuf[i][:]), r=[("dqbuf", i)], w=[("sdq", p)])

        def make_diag_scratch_conv(n):
            for _ in range(n):
                if not diag_tasks:
                    break
                cc = diag_tasks.pop(0)
                for hf in range(2):
                    i = dg_i[0]
                    dg_i[0] = (i + 1) % 2
                    for jl in range(16 if hf == 0 else 15):
                        j = hf * 16 + jl
                        S.dve(lambda e: e.tensor_scalar(out=dgbuf[i][:, jl * 128:(jl + 1) * 128], in0=identb[:], scalar1=C("convw", cc * 31 + j), scalar2=None, op0=ALU.mult), r=["identb", "cols"], w=[("dgbuf", i)])
                    if hf == 1:
                        S.dve(lambda e: e.memset(dgbuf[i][:, 15 * 128:16 * 128], 0.0), w=[("dgbuf", i)])
                    S.dma("sp", lambda e: e.dma_start(out=s_dg[2 * cc + hf], in_=dgbuf[i][:]), r=[("dgbuf", i)], w=[("sdg", 2 * cc + hf)])

        cvs = {}
        NPOOL = 0
        NDVE = 0

        def conv_begin():
            cvs["s1"] = nb(hold=True)
            cvs["s2"] = nb(hold=True)

        def conv_stage(cc):
            (bs1, ks1), (bs2, ks2) = cvs["s1"], cvs["s2"]
            bc, bck = nb()
            for hf in range(2):
                taps = [hf * 16 + jl for jl in range(16 if hf == 0 else 15) if hf * 16 + jl >= NPOOL + NDVE]
                if not taps:
                    continue
                dg_, dgk = dgload(cc, hf)
                for j in taps:
                    jl = j - hf * 16
                    S.pe(lambda e: e.matmul(bc[:], lhsT=dg_[:, jl * 128:(jl + 1) * 128], rhs=aT[:, cc, j:j + T], start=(j == NPOOL + NDVE), stop=(j == 30)), r=[dgk, ("aT", cc)], w=[bck])
            yc = big[:, cc * 512:(cc + 1) * 512]
            yk = [("big", 2 * cc), ("big", 2 * cc + 1)]
            sq, sqk = tmp()
            S.act(lambda e: e.activation(out=yc, in_=bc[:], func=AF.Identity, bias=C("convb", cc), scale=1.0), r=["cols"], w=yk + [bck])
            S.act(lambda e: e.activation(out=sq[:], in_=bc[:], func=AF.Square, bias=C("convb", cc), scale=1.0), r=["cols"], w=[sqk, bck])
            S.pe(lambda e: e.matmul(bs1[:], lhsT=onesf[:], rhs=yc, start=(cc == 0), stop=(cc == 7)), r=yk + ["onesf"], w=[ks1])
            S.pe(lambda e: e.matmul(bs2[:], lhsT=onesf[:], rhs=sq[:], start=(cc == 0), stop=(cc == 7)), r=[sqk, "onesf"], w=[ks2])
            S.pool(lambda e: e.tensor_copy(out=aT[:, cc, 0:30], in_=aT[:, cc, T:T + 30]), r=[("aT", cc)], w=[("aT", cc)])

        def conv_finish():
            (bs1, ks1), (bs2, ks2) = cvs["s1"], cvs["s2"]
            nm, nmk = tmp()
            S.act(lambda e: e.activation(out=mean_sb[:], in_=bs1[:], func=AF.Copy), w=["mean_sb", ks1])
            S.dve(lambda e: e.scalar_tensor_tensor(out=nm[:], in0=mean_sb[:], scalar=-1.0, in1=mean_sb[:], op0=ALU.mult, op1=ALU.mult), r=["mean_sb"], w=[nmk])
            S.dve(lambda e: e.tensor_tensor(out=nm[:], in0=bs2[:], in1=nm[:], op=ALU.add), r=[nmk], w=[nmk, ks2])
            rel(ks1, ks2)
            S.act(lambda e: e.activation(out=rstd_sb[:], in_=nm[:], func=AF.Sqrt, bias=EPS, scale=1.0), r=[nmk], w=["rstd_sb"])
            S.dve(lambda e: e.reciprocal(out=rstd_sb[:], in_=rstd_sb[:]), r=["rstd_sb"], w=["rstd_sb"])
            for cc in range(8):
                yc = big[:, cc * 512:(cc + 1) * 512]
                yk = [("big", 2 * cc), ("big", 2 * cc + 1)]
                S.dve(lambda e: e.tensor_tensor(out=yc, in0=yc, in1=mean_sb[:], op=ALU.subtract), r=yk + ["mean_sb"], w=yk)
                S.dve(lambda e: e.tensor_tensor(out=yc, in0=yc, in1=rstd_sb[:], op=ALU.mult), r=yk + ["rstd_sb"], w=yk)
                S.act(lambda e: e.activation(out=bigb[:, 8192 + cc * 512:8192 + (cc + 1) * 512], in_=yc, func=AF.Silu, scale=C("lng", cc), bias=C("lnb", cc)), r=yk + ["cols"], w=[("big", 16 + cc)])

        def glu_part(do_conv):
            for g in range(4):
                wa, wak = wload(s_in, IN_K, 0, 8, g * 256, 256)
                wgt, wgk = wload(s_in, IN_K, 0, 8, 1024 + g * 256, 256)
                for ml in range(2):
                    cc = 2 * g + ml
                    ba, bak = projA(wa, wak, ml)
                    bg, bgk = projA(wgt, wgk, ml)
                    sg, sgk = tmp()
                    S.act(lambda e: e.activation(out=sg[:], in_=bg[:], func=AF.Sigmoid, bias=C("b_glu", 8 + cc), scale=1.0), r=["cols"], w=[sgk, bgk])
                    S.dve(lambda e: e.scalar_tensor_tensor(out=aT[:, cc, 30:30 + T], in0=ba[:], scalar=C("b_glu", cc), in1=sg[:], op0=ALU.add, op1=ALU.mult), r=[sgk, "cols"], w=[("aT", cc), bak])
                    if not do_conv:
                        S.pool(lambda e: e.tensor_copy(out=aT[:, cc, 0:30], in_=aT[:, cc, T:T + 30]), r=[("aT", cc)], w=[("aT", cc)])

        kres = bigb[:, 4096:12288].rearrange("p (k n) -> p k n", k=8)
        KRES_K = [("big", i) for i in range(8, 24)]

        def qk_part(mcs, kb=8, res=False):
            def conv_stage(p):
                mc, qb, qbk, dq, dqk, ml = p
                oi = (kb + mc - 8) if mc >= 8 else mc
                b2, b2k = nb()
                for j in range(4):
                    S.pe(lambda e: e.matmul(b2[:], lhsT=dq[:, (ml * 4 + j) * 128:(ml * 4 + j + 1) * 128], rhs=qb[:, j:j + T], start=(j == 0), stop=(j == 3)), r=[dqk, qbk], w=[b2k])
                S.act(lambda e: e.activation(out=halo[:, mc * 3:mc * 3 + 3], in_=qb[:, T:T + 3], func=AF.Copy), r=[qbk, "halo"], w=["halo"])
                S.act(lambda e: e.activation(out=qkT[:, oi, :], in_=b2[:], func=AF.Silu, bias=C("qkb", mc), scale=1.0), r=["cols"], w=[("qkT", oi), b2k])

            pend = None
            for g0 in range(0, len(mcs), 2):
                mc0 = mcs[g0]
                if not res:
                    wv, wk = wload(s_in, IN_K, 0, 8, 2048 + mc0 * 128, 256)
                dq, dqk = dqload(mc0 // 2)
                for ml in range(2):
                    mc = mc0 + ml
                    if res:
                        b, bk = nb()
                        for kc in range(8):
                            S.pe(lambda e: e.matmul(b[:], lhsT=kres[:, kc, (mc - 8) * 128:(mc - 7) * 128], rhs=hT[:, kc, :], start=(kc == 0), stop=(kc == 7)), r=KRES_K + hT_all, w=[bk])
                    else:
                        b, bk = projA(wv, wk, ml)
                    r_ = qb_i[0]
                    qb_i[0] = (r_ + 1) % 3
                    qb, qbk = qkb16[r_], ("qkb16", r_)
                    S.act(lambda e: e.activation(out=qb[:, 3:3 + T], in_=b[:], func=AF.Identity, bias=C("b_qk", mc), scale=1.0), r=["cols"], w=[qbk, bk])
                    S.act(lambda e: e.activation(out=qb[:, 0:3], in_=halo[:, mc * 3:mc * 3 + 3], func=AF.Copy), r=["halo", qbk], w=[qbk])
                    if pend is not None:
                        conv_stage(pend)
                    pend = (mc, qb, qbk, dq, dqk, ml)
            conv_stage(pend)

        wg_done = [False]

        def gates_part():
            if not wg_done[0]:
                wg_done[0] = True
                S.dma("sp", lambda e: e.dma_start(out=wg[:], in_=s_in[:, 6144:6152].rearrange("(k p) n -> p k n", p=128)), r=in_keys(6144, 6152), w=["wg"])
            bi, bik = nb()
            bf_, bfk = nb()
            for kc in range(8):
                S.pe(lambda e, kc=kc: e.matmul(bi[0:4, :], lhsT=wg[:, kc, 0:4], rhs=hT[:, kc, :], start=(kc == 0), stop=(kc == 7)), r=["wg"] + hT_all, w=[bik])
            for kc in range(8):
                S.pe(lambda e, kc=kc: e.matmul(bf_[0:4, :], lhsT=wg[:, kc, 4:8], rhs=hT[:, kc, :], start=(kc == 0), stop=(kc == 7)), r=["wg"] + hT_all, w=[bfk])
            S.act(lambda e: e.activation(out=g_i[:], in_=bi[0:4, :], func=AF.Identity, bias=cols[0:4, _COLS["b_i"]:_COLS["b_i"] + 1], scale=1.0), r=["cols"], w=["g_i", bik])
            S.act(lambda e: e.activation(out=g_lf[:], in_=bf_[0:4, :], func=AF.Exp, bias=nbf_[:], scale=-1.0), r=["nbf_"], w=["g_lf", bfk])
            S.act(lambda e: e.activation(out=g_lf[:], in_=g_lf[:], func=AF.Ln, bias=1.0, scale=1.0), r=["g_lf"], w=["g_lf"])
            for j in range(4):
                S.dve(lambda e, j=j: e.tensor_tensor_scan(out=g_nb[:, j * 128:(j + 1) * 128], data0=g_one[:], data1=g_lf[:, j * 128:(j + 1) * 128], initial=0.0, op0=ALU.mult, op1=ALU.add), r=["g_one", "g_lf"], w=["g_nb"])
            S.dve(lambda e: e.tensor_tensor(out=g_i[:], in0=g_i[:], in1=g_nb[:], op=ALU.add), r=["g_i", "g_nb"], w=["g_i"])
            S.dve(lambda e: e.tensor_reduce(out=g_um[:], in_=g_i[:].rearrange("p (j t) -> p j t", j=4), axis=AX.X, op=ALU.max), r=["g_i"], w=["g_um"])
            for j in range(4):
                S.dve(lambda e, j=j: e.tensor_tensor(out=g_M[:, j:j + 1], in0=mst[:], in1=g_um[:, j:j + 1], op=ALU.max), r=["mst", "g_um"], w=["g_M"])
                S.dve(lambda e, j=j: e.tensor_tensor(out=g_dm[:, j:j + 1], in0=mst[:], in1=g_M[:, j:j + 1], op=ALU.subtract), r=["mst", "g_M"], w=["g_dm"])
                S.dve(lambda e, j=j: e.tensor_tensor(out=mst[:], in0=g_M[:, j:j + 1], in1=g_nb[:, j * 128 + 127:j * 128 + 128], op=ALU.subtract), r=["g_M", "g_nb"], w=["mst"])
            S.act(lambda e: e.activation(out=g_io[:], in_=g_dm[:], func=AF.Exp), r=["g_dm"], w=["g_io"])
            for j in range(4):
                S.dve(lambda e, j=j: e.tensor_scalar(out=g_t[:, j * 128:(j + 1) * 128], in0=g_i[:, j * 128:(j + 1) * 128], scalar1=g_M[:, j:j + 1], scalar2=-LN16, op0=ALU.subtract, op1=ALU.add), r=["g_i", "g_M"], w=["g_lf"])
            S.act(lambda e: e.activation(out=g_w[:], in_=g_t[:], func=AF.Exp), r=["g_lf"], w=["g_i"])
            for j in range(4):
                S.dve(lambda e, j=j: e.tensor_scalar(out=g_t[:, j * 128:(j + 1) * 128], in0=g_nb[:, j * 128:(j + 1) * 128], scalar1=g_M[:, j:j + 1], scalar2=None, op0=ALU.subtract), r=["g_nb", "g_M", "g_i"], w=["g_lf"])
            S.act(lambda e: e.activation(out=g_d[:], in_=g_t[:], func=AF.Exp), r=["g_lf"], w=["g_nb"])
            for h in range(4):
                S.dve(lambda e, h=h: e.tensor_scalar(out=g_iod[:].rearrange("p (j h) -> p j h", h=4)[:, :, h], in0=g_io[:], scalar1=ident[0:4, h:h + 1], scalar2=None, op0=ALU.mult), r=["g_io", "ident"], w=["g_iod"])
        def gates_B():
            bt, btk = nb()
            for j in range(4):
                S.pe(lambda e, j=j: e.transpose(out=bt[:, j * 8:j * 8 + 4], in_=g_w[:, j * 128:(j + 1) * 128], identity=ident[0:4, 0:4]), r=["g_i", "ident"], w=[btk])
                S.pe(lambda e, j=j: e.transpose(out=bt[:, j * 8 + 4:j * 8 + 8], in_=g_d[:, j * 128:(j + 1) * 128], identity=ident[0:4, 0:4]), r=["g_nb", "ident"], w=[btk])
            S.pe(lambda e: e.matmul(bt[:, 32:48], lhsT=ones1[0:4, :], rhs=g_iod[:], start=True, stop=True), r=["ones1", "g_iod"], w=[btk])
            S.dve(lambda e: e.tensor_copy(out=gcol[:], in_=bt[:, 0:32]), w=["gcol", btk])
            S.dve(lambda e: e.tensor_copy(out=iob[:], in_=bt[:, 32:48]), w=["iob", btk])

        vkey_main = lambda j: [("v", j)]
        vkey_alt = lambda j: [("hmT", jj) for jj in range(4)]
        vtok_alt = hmT[:].rearrange("p c t -> p (c t)").rearrange("p (j d) -> p j d", j=4)

        def v_part(vdst=None, vkeyf=None):
            if vdst is None:
                vdst, vkeyf = vtok, vkey_main
            for h in range(4):
                wv, wk = wload(s_in, IN_K, 0, 8, 4096 + h * 256, 256)
                for j in range(4):
                    b, bk = nb()
                    S.pe(lambda e, b=b, h=h: e.matmul(b[:, 0:256], lhsT=onesb[0:1, :], rhs=bvb[0:1, h * 256:(h + 1) * 256], start=True, stop=False), r=["onesb", "bvb"], w=[bk])
                    for kc in range(8):
                        S.pe(lambda e, b=b, kc=kc, j=j, wv=wv: e.matmul(b[:, 0:256], lhsT=hT[:, kc, j * 128:(j + 1) * 128], rhs=wv[:, kc, :], start=False, stop=(kc == 7)), r=[wk] + hT_all, w=[bk])
                    if (h + j) % 2 == 0:
                        S.act(lambda e, b=b, j=j, h=h: e.activation(out=vdst[:, j, h * 256:(h + 1) * 256], in_=b[:, 0:256], func=AF.Copy), w=vkeyf(j) + [bk])
                    else:
                        S.dve(lambda e, b=b, j=j, h=h: e.tensor_copy(out=vdst[:, j, h * 256:(h + 1) * 256], in_=b[:, 0:256]), w=vkeyf(j) + [bk])

        kt_i = [0]
        hm_i = [0]

        def ml_A(j, full, kb=8):
            cs = slice(j * 128, (j + 1) * 128)
            ctx = {"j": j, "cs": cs}
            bt, btk = nb()
            btb = bt[:].bitcast(BF16)
            for mc in range(8):
                S.pe(lambda e: e.transpose(out=btb[:, mc * 128:(mc + 1) * 128], in_=qkT[:, kb + mc, cs], identity=identb[:]), r=[("qkT", kb + mc), "identb"], w=[btk])
            ki = kt_i[0]
            kt_i[0] = (ki + 1) % 2
            kt, ktk = ktok[ki], ("ktok", ki)
            ctx["kt"], ctx["ktk"] = kt, ktk
            for h in range(4):
                S.act(lambda e: e.activation(out=kt[:, h * 256:(h + 1) * 256], in_=btb[:, h * 256:(h + 1) * 256], func=AF.Copy, scale=gcol[:, j * 8 + h:j * 8 + h + 1]), r=["gcol"], w=[ktk, btk])
            if full:
                bsc, bsck = nb()
                for h in range(4):
                    for dc in range(2):
                        S.pe(lambda e: e.matmul(bsc[:, h * 128:(h + 1) * 128], lhsT=qkT[:, 8 + 2 * h + dc, cs], rhs=qkT[:, 2 * h + dc, cs], start=(dc == 0), stop=(dc == 1)), r=[("qkT", 8 + 2 * h + dc), ("qkT", 2 * h + dc)], w=[bsck])
                scd, scdk = tmp()
                scdb = scd[:].bitcast(BF16)
                ctx["scdb"], ctx["scdk"] = scdb, scdk
                for h in range(4):
                    S.dve(lambda e: e.scalar_tensor_tensor(out=scdb[:, h * 128:(h + 1) * 128], in0=bsc[:, h * 128:(h + 1) * 128], scalar=gcol[:, j * 8 + h:j * 8 + h + 1], in1=maskT[:], op0=ALU.mult, op1=ALU.mult), r=["gcol", "maskT"], w=[scdk, bsck])
            return ctx

        def ml_B(ctx, full, vsrc=None, vkeyf=None):
            if vsrc is None:
                vsrc, vkeyf = vtok, vkey_main
            j, cs, kt, ktk = ctx["j"], ctx["cs"], ctx["kt"], ctx["ktk"]
            if full:
                scdb, scdk = ctx["scdb"], ctx["scdk"]
                for h in range(4):
                    S.act(lambda e: e.activation(out=Cbf[:, h, :], in_=Cst[:, h, :], func=AF.Copy, scale=iob[:, j * 4 + h:j * 4 + h + 1]), r=[("C", h), "iob"], w=[("Cbf", h)])
                    S.dve(lambda e: e.tensor_scalar(out=nbf[:, 2 * h:2 * h + 2], in0=nst[:, 2 * h:2 * h + 2], scalar1=iob[:, j * 4 + h:j * 4 + h + 1], scalar2=None, op0=ALU.mult), r=["nst", "iob"], w=["nbf"])
                bnA, bnAk = nb(hold=True)
                bnB, bnBk = nb(hold=True)
                ctx["bn"] = (bnA, bnAk, bnB, bnBk)
            bq, bqk = nb(hold=True)
            if full:
                for h in range(4):
                    bn, bnk = (bnA, bnAk) if h < 2 else (bnB, bnBk)
                    osl = slice((h % 2) * 256, (h % 2 + 1) * 256)
                    S.pe(lambda e: e.matmul(bn[:, osl], lhsT=scdb[:, h * 128:(h + 1) * 128], rhs=vtok[:, j, h * 256:(h + 1) * 256], start=True, stop=False), r=[scdk, ("v", j)], w=[bnk])
                    for dc in range(2):
                        S.pe(lambda e: e.matmul(bn[:, osl], lhsT=qkT[:, 2 * h + dc, cs], rhs=Cbf[:, h, dc * 256:(dc + 1) * 256], start=False, stop=(dc == 1)), r=[("qkT", 2 * h + dc), ("Cbf", h)], w=[bnk])
                    S.pe(lambda e: e.matmul(bq[:, h:h + 1], lhsT=scdb[:, h * 128:(h + 1) * 128], rhs=onesb[:, 0:1], start=True, stop=False), r=[scdk, "onesb"], w=[bqk])
                    for dc in range(2):
                        S.pe(lambda e: e.matmul(bq[:, h:h + 1], lhsT=qkT[:, 2 * h + dc, cs], rhs=nbf[:, 2 * h + dc:2 * h + dc + 1], start=False, stop=(dc == 1)), r=[("qkT", 2 * h + dc), "nbf"], w=[bqk])
            for h in range(4):
                bd, bdk = nb()
                for dc in range(2):
                    S.pe(lambda e: e.matmul(bd[:, dc * 256:(dc + 1) * 256], lhsT=kt[:, h * 256 + dc * 128:h * 256 + (dc + 1) * 128], rhs=vsrc[:, j, h * 256:(h + 1) * 256], start=True, stop=True), r=[ktk] + vkeyf(j), w=[bdk])
                    S.pe(lambda e: e.matmul(bq[:, 8 + 2 * h + dc:8 + 2 * h + dc + 1], lhsT=kt[:, h * 256 + dc * 128:h * 256 + (dc + 1) * 128], rhs=onesb[:, 0:1], start=True, stop=True), r=[ktk, "onesb"], w=[bqk])
                S.dve(lambda e: e.scalar_tensor_tensor(out=Cst[:, h, :], in0=Cst[:, h, :], scalar=iob[:, j * 4 + h:j * 4 + h + 1], in1=bd[:], op0=ALU.mult, op1=ALU.add), r=[("C", h), "iob"], w=[("C", h), bdk])
            if full:
                S.dve(lambda e: e.tensor_copy(out=sm[:, 0:4], in_=bq[:, 0:4]), w=["sm", bqk])
            for h in range(4):
                S.dve(lambda e: e.scalar_tensor_tensor(out=nst[:, 2 * h:2 * h + 2], in0=nst[:, 2 * h:2 * h + 2], scalar=iob[:, j * 4 + h:j * 4 + h + 1], in1=bq[:, 8 + 2 * h:8 + 2 * h + 2], op0=ALU.mult, op1=ALU.add), r=["nst", "iob"], w=["nst", bqk])
            rel(bqk)

        def ml_Cpre(ctx):
            j = ctx["j"]
            bnA, bnAk, bnB, bnBk = ctx["bn"]
            for h in range(4):
                bn, bnk = (bnA, bnAk) if h < 2 else (bnB, bnBk)
                osl = slice((h % 2) * 256, (h % 2 + 1) * 256)
                S.dve(lambda e: e.bn_stats(out=st6[:, h, :], in_=bn[:, osl]), w=["st6", bnk])
            for h in range(4):
                S.dve(lambda e: e.bn_aggr(out=mv[:, h, :], in_=st6[:, h, :]), r=["st6"], w=["mv"])
            S.dve(lambda e: e.tensor_scalar(out=sm[:, 4:8], in0=sm[:, 0:4], scalar1=-1.0, scalar2=None, op0=ALU.mult), r=["sm"], w=["sm"])
            S.dve(lambda e: e.tensor_tensor(out=sm[:, 4:8], in0=sm[:, 4:8], in1=sm[:, 0:4], op=ALU.max), r=["sm"], w=["sm"])
            S.dve(lambda e: e.tensor_tensor(out=sm[:, 8:12], in0=sm[:, 4:8], in1=gcol[:, j * 8 + 4:j * 8 + 8], op=ALU.max), r=["sm", "gcol"], w=["sm"])
            S.dve(lambda e: e.tensor_tensor(out=sm[:, 12:16], in0=sm[:, 8:12], in1=sm[:, 8:12], op=ALU.mult), r=["sm"], w=["sm"])
            S.dve(lambda e: e.scalar_tensor_tensor(out=sm[:, 16:20], in0=sm[:, 12:16], scalar=EPS, in1=mv[:, :, 1], op0=ALU.mult, op1=ALU.add), r=["sm", "mv"], w=["sm"])
            S.act(lambda e: e.activation(out=sm[:, 20:24], in_=sm[:, 16:20], func=AF.Sqrt), r=["sm"], w=["sm"])
            S.dve(lambda e: e.reciprocal(out=sm[:, 24:28], in_=sm[:, 20:24]), r=["sm"], w=["sm"])
            S.dve(lambda e: e.scalar_tensor_tensor(out=sm[:, 28:32], in0=mv[:, :, 0], scalar=-1.0, in1=sm[:, 24:28], op0=ALU.mult, op1=ALU.mult), r=["sm", "mv"], w=["sm"])
            hi = hm_i[0]
            hm_i[0] = (hi + 1) % 2
            hm, hmk = hmtok[hi], ("hmtok", hi)
            ctx["hm"], ctx["hmk"] = hm, hmk
            for h in range(4):
                bn, bnk = (bnA, bnAk) if h < 2 else (bnB, bnBk)
                osl = slice((h % 2) * 256, (h % 2 + 1) * 256)
                S.act(lambda e: e.activation(out=hm[:, h * 256:(h + 1) * 256], in_=bn[:, osl], func=AF.Identity, scale=sm[:, 24 + h:25 + h], bias=sm[:, 28 + h:29 + h]), r=["sm"], w=[hmk, bnk])
            rel(bnAk, bnBk)

        def ml_Ctr(ctx):
            j, cs, hm, hmk = ctx["j"], ctx["cs"], ctx["hm"], ctx["hmk"]
            bh, bhk = nb()
            bhb = bh[:].bitcast(BF16)
            for cc in range(8):
                S.pe(lambda e: e.transpose(out=bhb[:, cc * 128:(cc + 1) * 128], in_=hm[:, cc * 128:(cc + 1) * 128], identity=identb[:]), r=[hmk, "identb"], w=[bhk])
            S.dve(lambda e: e.tensor_copy(out=hmT[:, :, cs], in_=bhb.rearrange("p (c t) -> p c t", c=8)), w=[("hmT", j), bhk])

        def mlstm_all(full, kb=8, vsrc=None, vkeyf=None):
            ctxs = [None] * 4
            if full:
                conv_begin()
            ctxs[0] = ml_A(0, full, kb)
            for j in range(4):
                if j < 3:
                    ctxs[j + 1] = ml_A(j + 1, full, kb)
                ml_B(ctxs[j], full, vsrc, vkeyf)
                if full:
                    conv_stage(2 * j)
                    ml_Cpre(ctxs[j])
                    conv_stage(2 * j + 1)
                    if j > 0:
                        ml_Ctr(ctxs[j - 1])
            if full:
                ml_Ctr(ctxs[3])
                conv_finish()

        def merge_part():
            hmT_all = [("hmT", j) for j in range(4)]
            a2v = bigb[:, 8192:12288].rearrange("p (k t) -> p k t", k=8)
            a2k = [("big", 16 + k) for k in range(8)]
            for g in range(4):
                wo, wok = wload(s_in, IN_K, 0, 8, 5120 + g * 256, 256)
                wc, wck = wload(s_in, IN_K, 0, 8, 6152 + g * 256, 256)
                wm, wmk = wload(s_in, IN_K, 0, 8, 7176 + g * 256, 256)
                wp, wpk = wload(s_pw, scr_keys["pw"], 0, 8, g * 256, 256)
                st_ = []
                for ml in range(2):
                    cc = 2 * g + ml
                    bo, bok = projA(wo, wok, ml)
                    bm, bmk = projA(wm, wmk, ml)
                    so, sok = tmp()
                    S.act(lambda e: e.activation(out=so[:], in_=bo[:], func=AF.Sigmoid, bias=C("b_o", cc), scale=1.0), r=["cols"], w=[sok, bok])
                    S.act(lambda e: e.activation(out=ptmp[:], in_=bm[:], func=AF.Sigmoid, bias=C("b_mm", cc), scale=1.0), r=["cols"], w=["ptmp", bmk])
                    S.dve(lambda e: e.tensor_tensor(out=so[:], in0=so[:], in1=ptmp[:], op=ALU.mult), r=[sok, "ptmp"], w=[sok])
                    S.dve(lambda e: e.scalar_tensor_tensor(out=so[:], in0=hmT[:, cc, :], scalar=C("mng", cc), in1=so[:], op0=ALU.mult, op1=ALU.mult), r=[sok, "cols"] + hmT_all, w=[sok])
                    bc, bck = projA(wc, wck, ml)
                    sgc, sgck = tmp()
                    S.act(lambda e: e.activation(out=sgc[:], in_=bc[:], func=AF.Sigmoid, bias=C("b_mc", cc), scale=1.0), r=["cols"], w=[sgck, bck])
                    st_.append((cc, ml, so, sok, sgc, sgck))
                for cc, ml, so, sok, sgc, sgck in st_:
                    bp, bpk = projA(wp, wpk, ml, rhs_keys=a2k, rhs=a2v)
                    S.dve(lambda e: e.scalar_tensor_tensor(out=sgc[:], in0=bp[:], scalar=C("b_pw", cc), in1=sgc[:], op0=ALU.add, op1=ALU.mult), r=[sgck, "cols"], w=[sgck, bpk])
                    S.dve(lambda e: e.tensor_tensor(out=qkT[:, 8 + cc, :], in0=sgc[:], in1=so[:], op=ALU.add), r=[sgck, sok], w=[("qkT", 8 + cc)])

        def resid_proj(kind, xb, hook=None):
            xt = xts[xb]
            for cb in range(2):
                bj = [nb(hold=True) for _ in range(4)]
                if kind == "out":
                    for half in range(2):
                        wv, wk = wload(s_out, [("scr", "out", rb, cb) for rb in range(half * 4, half * 4 + 4)], half * 512, 4, cb * 512, 512)
                        for kl in range(4):
                            kc = half * 4 + kl
                            for j in range(4):
                                b, bk = bj[j]
                                S.pe(lambda e: e.matmul(b[:], lhsT=qkT[:, 8 + kc, j * 128:(j + 1) * 128], rhs=wv[:, kl, :], start=(kc == 0), stop=(kc == 7)), r=[wk, ("qkT", 8 + kc)], w=[bk])
                else:
                    for fp in range(11):
                        wv, wk = wload(s_dn, [("scr", "dn", rb, cb) for rb in (2 * fp, 2 * fp + 1)], fp * 256, 2, cb * 512, 512)
                        for fl in range(2):
                            f = 2 * fp + fl
                            for j in range(4):
                                b, bk = bj[j]
                                S.pe(lambda e: e.matmul(b[:], lhsT=bigb[:, f * 512 + j * 128:f * 512 + (j + 1) * 128], rhs=wv[:, fl, :], start=(f == 0), stop=(f == 21)), r=[wk, ("big", f)], w=[bk])
                if cb == 1 and hook is not None:
                    hook()
                for j in range(4):
                    b, bk = bj[j]
                    S.dve(lambda e: e.tensor_tensor(out=xt[:, j, cb * 512:(cb + 1) * 512], in0=b[:], in1=xt[:, j, cb * 512:(cb + 1) * 512], op=ALU.add), r=[("x", xb, j)], w=[("x", xb, j), bk])
                rel(*[k for _, k in bj])

        def ffn_up():
            for fp in range(11):
                wgv, wgk = wload(s_up, scr_keys["up"], 0, 8, fp * 256, 256)
                wuv, wuk = wload(s_up, scr_keys["up"], 0, 8, DFF + fp * 256, 256)
                for fl in range(2):
                    f = 2 * fp + fl
                    hk = [("hmT", j) for j in range(4)]
                    bg, bgk = projA(wgv, wgk, fl, rhs_keys=hk, rhs=hmT)
                    bu, buk = projA(wuv, wuk, fl, rhs_keys=hk, rhs=hmT)
                    sg, sgk = tmp()
                    S.act(lambda e, sg=sg, bg=bg: e.activation(out=sg[:], in_=bg[:], func=AF.Silu), w=[sgk, bgk])
                    S.dve(lambda e, sg=sg, bu=bu, f=f: e.tensor_tensor(out=bigb[:, f * 512:(f + 1) * 512], in0=bu[:], in1=sg[:], op=ALU.mult), r=[sgk], w=[("big", f), buk])

        def final_norm(st, xb):
            xt = xts[xb]
            rms_stats(xb, 2)
            rstd = rstds[2]
            for j in range(4):
                S.dve(lambda e: e.scalar_tensor_tensor(out=xt[:, j, :], in0=xt[:, j, :], scalar=rstd[:, j:j + 1], in1=Gf[:], op0=ALU.mult, op1=ALU.mult), r=[("x", xb, j), ("rstd", 2), "Gf"], w=[("x", xb, j)])
            S.dma("sp", lambda e: e.dma_start(out=out_d[st * T:(st + 1) * T, :].rearrange("(j p) d -> p j d", p=128), in_=xt[:]), r=[("x", xb, j) for j in range(4)], out=True)

        def load_x(src, st, xb):
            S.dma("sp", lambda e: e.dma_start(out=xts[xb][:], in_=src[st * T:(st + 1) * T, :].rearrange("(j p) d -> p j d", p=128)), w=[("x", xb, j) for j in range(4)])

        n_rest = (len(rest_list) + max(NP - 1, 1) - 1) // max(NP - 1, 1)
        def xsrc(g):
            return (xp, g) if g < NP else (xm, g - NP)

        def par(st):
            if (NP - 1 - st) % 2 == 0:
                return 8, vtok, vkey_main
            return 0, vtok_alt, vkey_alt

        def front_prev_pieces(st):
            kb, vdst, vkeyf = par(st)
            return [
                lambda: (cast_rest(st, n_rest, st % 2), rmsnorm_T(G1c, 0, st % 2)),
                lambda: qk_part(list(range(8, 12)), kb=kb, res=True),
                lambda: qk_part(list(range(12, 16)), kb=kb, res=True),
                lambda: (gates_part(), v_part(vdst, vkeyf)),
            ]

        def mlstm_prev_steps(kb, vsrc, vkeyf):
            ctxs = [None] * 4

            def step(j):
                if j == 0:
                    ctxs[0] = ml_A(0, False, kb)
                if j < 3:
                    ctxs[j + 1] = ml_A(j + 1, False, kb)
                ml_B(ctxs[j], False, vsrc, vkeyf)
            return [lambda j=j: step(j) for j in range(4)]

        load_x(*xsrc(0), 0)
        for hf_ in range(2):
            S.dma("sp", lambda e: e.dma_start(out=kres[:, :, hf_ * 512:(hf_ + 1) * 512], in_=s_in[:, 3072 + hf_ * 512:3072 + (hf_ + 1) * 512].rearrange("(k p) n -> p k n", p=128)), r=in_keys(3072, 4096), w=KRES_K)
        load_x(*xsrc(1), 1)
        make_diag_scratch_qk()
        mod_part1()
        for p_ in front_prev_pieces(0):
            p_()
        mod_part2()
        for st in range(NP):
            last = st == NP - 1
            kb, vsrc, vkeyf = par(st)
            gates_B()
            if st + 2 <= NP:
                load_x(*xsrc(st + 2), st % 2)
            steps = mlstm_prev_steps(kb, vsrc, vkeyf)
            if not last:
                pieces = front_prev_pieces(st + 1)
                pieces[0]()
                steps[0]()
                pieces[1]()
                steps[1]()
                pieces[2]()
                steps[2]()
                pieces[3]()
                scaled_scratch((30 + NP - 1) // NP)
                make_diag_scratch_conv((8 + NP - 1) // NP)
                steps[3]()
            else:
                scaled_scratch((30 + NP - 1) // NP)
                make_diag_scratch_conv((8 + NP - 1) // NP)
                glu_part(False)
                qk_part(list(range(0, 8)))
                for sp_ in steps:
                    sp_()
        xi = NP
        fl = C("flag")
        for h in range(4):
            S.dve(lambda e: e.tensor_scalar(out=Cst[:, h, :], in0=Cst[:, h, :], scalar1=fl, scalar2=None, op0=ALU.mult), r=[("C", h), "cols"], w=[("C", h)])
        S.dve(lambda e: e.tensor_scalar(out=nst[:], in0=nst[:], scalar1=fl, scalar2=None, op0=ALU.mult), r=["nst", "cols"], w=["nst"])
        S.dve(lambda e: e.tensor_scalar(out=mst[:], in0=mst[:], scalar1=cols[0:4, _COLS["flag"]:_COLS["flag"] + 1], scalar2=None, op0=ALU.mult), r=["mst", "cols"], w=["mst"])
        S.dve(lambda e: e.tensor_scalar(out=halo[:], in0=halo[:], scalar1=fl, scalar2=None, op0=ALU.mult), r=["halo", "cols"], w=["halo"])
        for cc in range(8):
            S.dve(lambda e: e.tensor_scalar(out=aT[:, cc, 0:30], in0=aT[:, cc, 0:30], scalar1=fl, scalar2=None, op0=ALU.mult), r=[("aT", cc), "cols"], w=[("aT", cc)])

        def front(xb):
            rmsnorm_T(G1c, 0, xb)
            qk_part(list(range(8, 16)))
            gates_part()
            v_part()
            gates_B()

        for src_, dst_, k in up_list:
            S.dma("pool", lambda e: e.dma_start(out=dst_, in_=src_), w=[k])
        front(xi % 2)
        pending_final = None
        for st in range(NS):
            xb = xi % 2
            nxt = st + 1 < NS
            qk_part(list(range(0, 8)))
            glu_part(True)
            if pending_final is not None:
                final_norm(*pending_final)
                pending_final = None
            if nxt:
                load_x(xm, st + 1, (xi + 1) % 2)
            mlstm_all(True)
            if nxt:
                rms_stats((xi + 1) % 2, 0)
            merge_part()
            resid_proj("out", xb)
            if nxt:
                front((xi + 1) % 2)
            rmsnorm_T(G2c, 16, xb, dst=hmT)
            ffn_up()
            resid_proj("dn", xb)
            pending_final = (st, xb)
            xi += 1
        final_norm(*pending_final)

        S.emit(block, sems, dsems)
    return nc


def _colform(v, n):
    buf = np.zeros(n * 128, np.float32)
    buf[:v.size] = np.asarray(v, np.float32).ravel()
    return np.ascontiguousarray(buf.reshape(n, 128).T)


_NC_CACHE = {}
_PREP_ONLY = [False]
_DBG = {}


def kernel(x, c, w_ada, b_ada, norm_mix_g, w_in, b_in, conv_dw_w, conv_dw_b, conv_ln_g, conv_ln_b, w_conv_pw,
           b_conv_pw, qk_conv_w, qk_conv_b, mlstm_norm_g, w_out, norm_ffn_g, w_ffn_up, w_ffn_down, final_norm_g):
    x = np.asarray(x, np.float32)
    B, SEQ, _ = x.shape
    half = SEQ // 2
    NS = half // T
    NP = NS
    f = lambda a: np.ascontiguousarray(np.asarray(a, np.float32))
    b_in0 = f(b_in)[0]
    bada = f(b_ada)[0]
    common = np.zeros((128, NCOLS), np.float32)

    def put(name, arr):
        common[:, _COLS[name]:_COLS[name] + arr.shape[1]] = arr

    put("b_glu", _colform(b_in0[0:2048], 16))
    put("b_qk", _colform(b_in0[2048:4096], 16))
    put("b_o", _colform(b_in0[5120:6144], 8))
    put("b_mc", _colform(b_in0[6152:7176], 8))
    put("b_mm", _colform(b_in0[7176:8200], 8))
    cw = f(conv_dw_w)[0]
    put("convw", np.ascontiguousarray(cw.reshape(31, 8, 128).transpose(2, 1, 0).reshape(128, 248)))
    put("convb", _colform(f(conv_dw_b)[0], 8))
    put("lng", _colform(f(conv_ln_g)[0], 8))
    put("lnb", _colform(f(conv_ln_b)[0], 8))
    put("b_pw", _colform(f(b_conv_pw)[0], 8))
    qw = f(qk_conv_w)[0]
    put("qkw", np.ascontiguousarray(qw.reshape(4, 16, 128).transpose(2, 1, 0).reshape(128, 64)))
    put("qkb", _colform(f(qk_conv_b)[0], 16))
    put("mng", _colform(f(mlstm_norm_g)[0], 8))
    put("nmg", _colform(f(norm_mix_g)[0], 8))
    put("nfg", _colform(f(norm_ffn_g)[0], 8))
    put("bada", np.concatenate([_colform(bada[0:1024], 8), _colform(bada[1024:2048], 8),
                                _colform(bada[3072:4096], 8), _colform(bada[4096:5120], 8)], axis=1))
    gi = np.zeros((128, 1), np.float32)
    gi[0:4, 0] = b_in0[6144:6148]
    gf = np.zeros((128, 1), np.float32)
    gf[0:4, 0] = b_in0[6148:6152]
    put("b_i", gi)
    put("b_f", gf)
    rows = np.ascontiguousarray(np.stack([bada[2048:3072], bada[5120:6144], f(final_norm_g), b_in0[4096:5120]]))
    shared = {"rows": rows, "w_ada": f(w_ada)[0], "w_in": f(w_in)[0], "w_pw": f(w_conv_pw)[0], "w_out": f(w_out)[0],
              "w_up": f(w_ffn_up)[0], "w_dn": f(w_ffn_down)[0]}
    in_maps = []
    cf = f(c)
    for core in range(2 * B):
        b, hf = core // 2, core % 2
        cols = common.copy()
        cols[:, _COLS["ccol"]:_COLS["ccol"] + 8] = _colform(cf[b], 8)
        cols[:, _COLS["flag"]] = float(hf)
        m = dict(shared)
        m["cols"] = cols
        m["xm"] = np.ascontiguousarray(x[b, hf * half:(hf + 1) * half])
        m["xp"] = np.ascontiguousarray(x[b, 0:half])
        in_maps.append(m)
    if _PREP_ONLY[0]:
        return in_maps, (NS, NP)
    key = (NS, NP)
    if key not in _NC_CACHE:
        _NC_CACHE[key] = build(NS, NP)
    nc = _NC_CACHE[key]
    res = run_bass_kernel_spmd(nc, in_maps, core_ids=list(range(2 * B)))
    out = np.empty((B, SEQ, D), np.float32)
    for core in range(2 * B):
        b, hf = core // 2, core % 2
        out[b, hf * half:(hf + 1) * half] = np.asarray(res.results[core]["out"], np.float32)
    return out
```

```python
import math
from contextlib import ExitStack

import numpy as np
import concourse.bass as bass
import concourse.mybir as mybir
from concourse.bass_utils import run_bass_kernel_spmd

F32 = mybir.dt.float32
BF16 = mybir.dt.bfloat16
AF = mybir.ActivationFunctionType
ALU = mybir.AluOpType
AX = mybir.AxisListType

D = 1024
NIN = 8200
DFF = 2816
T = 512
EPS = 1e-6
LN16 = math.log(16.0)


class Op:
    __slots__ = ("eng", "fn", "r", "w", "dma", "deps", "sig", "cnt", "dsem", "dval", "dprev", "out")

    def __init__(self, eng, fn, r, w, dma, out):
        self.eng = eng
        self.fn = fn
        self.r = tuple(r)
        self.w = tuple(w)
        self.dma = dma
        self.out = out
        self.deps = {}
        self.sig = False
        self.cnt = 0
        self.dsem = None
        self.dval = 0
        self.dprev = 0


class _Rec:
    def __getattr__(self, name):
        return lambda *a, **k: (name, a, k)


_REC = _Rec()


class Sched:
    ENGS = ("pe", "act", "dve", "pool", "sp")

    def __init__(self, n_dma_sems):
        self.ops = []
        self.n_dma_sems = n_dma_sems

    def add(self, eng, fn, r=(), w=(), dma=False, out=False):
        self.ops.append(Op(eng, fn(_REC), r, w, dma, out))

    def pe(self, fn, r=(), w=()):
        self.add("pe", fn, r, w)

    def act(self, fn, r=(), w=()):
        self.add("act", fn, r, w)

    def dve(self, fn, r=(), w=()):
        self.add("dve", fn, r, w)

    def pool(self, fn, r=(), w=()):
        self.add("pool", fn, r, w)

    def dma(self, q, fn, r=(), w=(), out=False):
        self.add(q, fn, r, w, dma=True, out=out)

    def analyze(self):
        ops = self.ops
        last_w = {}
        readers = {}
        for i, op in enumerate(ops):
            deps = {}
            for k in op.r:
                if k in last_w:
                    deps[last_w[k]] = True
            for k in op.w:
                if k in last_w:
                    deps[last_w[k]] = True
                for j in readers.get(k, ()):
                    deps.setdefault(j, False)
            for k in op.r:
                readers.setdefault(k, []).append(i)
            for k in op.w:
                last_w[k] = i
                readers[k] = []
            deps.pop(i, None)
            need = {}
            for d, raw in deps.items():
                dop = ops[d]
                if dop.dma:
                    need[d] = True
                elif dop.eng == op.eng and not op.dma:
                    if op.eng != "pe":
                        need[d] = True
                else:
                    need[d] = True
            op.deps = need
            for d in need:
                ops[d].sig = True
        cnt = {e: 0 for e in self.ENGS}
        dma_i = {e: 0 for e in self.ENGS}
        dma_last = {}
        for op in ops:
            if op.dma:
                k = (op.eng, dma_i[op.eng] % self.n_dma_sems[op.eng])
                dma_i[op.eng] += 1
                op.dsem = k
                op.dprev = dma_last.get(k, 0)
                op.dval = op.dprev + 16
                dma_last[k] = op.dval
            elif op.sig:
                cnt[op.eng] += 1
                op.cnt = cnt[op.eng]

    def emit(self, block, sems, dsems):
        ops = self.ops
        self.analyze()
        out_waits = {}
        for op in ops:
            if op.dma and op.out:
                out_waits[op.dsem] = max(out_waits.get(op.dsem, 0), op.dval)

        def run_engine(ename, eng):
            waited = {}

            def wait(sem_key, sem, val):
                if waited.get(sem_key, 0) >= val:
                    return
                waited[sem_key] = val
                eng.wait_ge(sem, val)

            for op in ops:
                if op.eng != ename:
                    continue
                for d in sorted(op.deps):
                    dop = ops[d]
                    if dop.dma:
                        wait(dop.dsem, dsems[dop.dsem], dop.dval)
                    else:
                        wait(dop.eng, sems[dop.eng], dop.cnt)
                if op.dma:
                    if op.dprev > 0:
                        wait(op.dsem, dsems[op.dsem], op.dprev)
                    name, a, k = op.fn
                    getattr(eng, name)(*a, **k).then_inc(dsems[op.dsem], 16)
                else:
                    name, a, k = op.fn
                    ins = getattr(eng, name)(*a, **k)
                    if op.sig:
                        ins.then_inc(sems[ename], 1)
            if ename == "sp":
                for k, v in out_waits.items():
                    wait(k, dsems[k], v)

        block.tensor(lambda e: run_engine("pe", e))
        block.scalar(lambda e: run_engine("act", e))
        block.vector(lambda e: run_engine("dve", e))
        block.gpsimd(lambda e: run_engine("pool", e))
        block.sync(lambda e: run_engine("sp", e))


_COLS = {}
_off = 0
for _n, _w in [("b_glu", 16), ("b_qk", 16), ("b_o", 8), ("b_mc", 8), ("b_mm", 8), ("convw", 248), ("convb", 8),
               ("lng", 8), ("lnb", 8), ("b_pw", 8), ("qkw", 64), ("qkb", 16), ("mng", 8), ("nmg", 8), ("nfg", 8),
               ("bada", 32), ("ccol", 8), ("flag", 1), ("b_i", 1), ("b_f", 1)]:
    _COLS[_n] = _off
    _off += _w
NCOLS = _off


def build(NS, NP):
    nc = bass.Bass("TRN2", target_bir_lowering=False)
    NM, NPT = NS * T, NP * T

    def din(name, shape):
        return nc.dram_tensor(name, shape, F32, kind="ExternalInput").ap()

    xm = din("xm", [NM, D])
    xp = din("xp", [NPT, D])
    cols_d = din("cols", [128, NCOLS])
    rows_d = din("rows", [4, D])
    w_ada = din("w_ada", [D, 6 * D])
    w_in = din("w_in", [D, NIN])
    w_pw = din("w_pw", [D, D])
    w_out = din("w_out", [D, D])
    w_up = din("w_up", [D, 2 * DFF])
    w_dn = din("w_dn", [DFF, D])
    out_d = nc.dram_tensor("out", [NM, D], F32, kind="ExternalOutput").ap()
    s_in = nc.dram_tensor("s_in", [D, NIN], BF16).ap()
    s_pw = nc.dram_tensor("s_pw", [D, D], BF16).ap()
    s_out = nc.dram_tensor("s_out", [D, D], BF16).ap()
    s_up = nc.dram_tensor("s_up", [D, 2 * DFF], BF16).ap()
    s_dn = nc.dram_tensor("s_dn", [DFF, D], BF16).ap()
    s_dg = nc.dram_tensor("s_dg", [16, 128, 16 * 128], BF16).ap()
    s_dq = nc.dram_tensor("s_dq", [8, 128, 1024], BF16).ap()

    NDS = {"sp": 16, "pool": 8, "act": 1, "pe": 1, "dve": 1}
    S = Sched(NDS)

    with ExitStack() as es:
        def sb(name, shape, dt=F32):
            return es.enter_context(nc.sbuf_tensor(name, shape, dt))

        sems = {e: es.enter_context(nc.semaphore(f"s_{e}")) for e in Sched.ENGS}
        dsems = {}
        for e in ("sp", "pool"):
            for i in range(NDS[e]):
                dsems[(e, i)] = es.enter_context(nc.semaphore(f"d_{e}{i}"))

        banks = [es.enter_context(nc.psum_tensor(f"bank{i}", [128, 512], F32)) for i in range(8)]
        bank_i = [0]

        held = set()

        def nb(hold=False):
            for _ in range(9):
                i = bank_i[0]
                bank_i[0] = (i + 1) % 8
                if i not in held:
                    break
            else:
                raise RuntimeError("no free PSUM bank")
            if hold:
                held.add(i)
            return banks[i], ("bank", i)

        def rel(*keys):
            for k in keys:
                held.discard(k[1])

        cols = sb("cols_sb", [128, NCOLS])
        ident = sb("ident", [128, 128])
        identb = sb("identb", [128, 128], BF16)
        onesf = sb("onesf", [128, 128])
        ones1 = sb("ones1", [128, 128])
        onesb = sb("onesb", [128, 128], BF16)
        maskT = sb("maskT", [128, 128])
        Gf = sb("Gf", [128, D])
        bvb = sb("bvb", [1, D], BF16)
        modc = sb("modc", [128, 32])
        G1c = sb("G1c", [128, 8])
        G2c = sb("G2c", [128, 8])
        scc = sb("scc", [128, 8])
        modr = sb("modr", [128, 32])
        nbf_ = sb("nbf_", [4, 1])
        wg = sb("wg", [128, 8, 8], BF16)
        xts = [sb(f"xt{i}", [128, 4, D]) for i in range(2)]
        xsb = [sb(f"xsb{i}", [128, D], BF16) for i in range(2)]
        junk = xsb[1]
        ptmp = sb("ptmp", [128, T])
        ssqs = [sb(f"ssq{i}", [128, 4]) for i in range(3)]
        rstds = [sb(f"rstd{i}", [128, 4]) for i in range(3)]
        sqj = sb("sqj", [128, D], BF16)
        dg = [sb(f"dg{i}", [128, 128]) for i in range(4)]
        hT = sb("hT", [128, 8, T], BF16)
        aT = sb("aT", [128, 8, T + 30], BF16)
        big = sb("big", [128, 6144])
        bigb = big[:].bitcast(BF16)
        G1g = big[:, 0:D]
        G2g = big[:, D:2 * D]
        G1K = [("big", i) for i in range(0, 4)]
        G2K = [("big", i) for i in range(4, 8)]
        qkb16 = [sb(f"qkb16_{i}", [128, T + 4], BF16) for i in range(3)]
        halo = sb("halo", [128, 48], BF16)
        dgbuf = [sb(f"dgbuf{i}", [128, 16 * 128], BF16) for i in range(2)]
        dqbuf = [sb(f"dqbuf{i}", [128, 1024], BF16) for i in range(2)]
        qkT = sb("qkT", [128, 16, T], BF16)
        ktok = [sb(f"ktok{i}", [128, D], BF16) for i in range(2)]
        vtok = sb("vtok", [128, 4, D], BF16)
        tmpf = [sb(f"tmpf{i}", [128, T]) for i in range(5)]
        Cst = sb("Cst", [128, 4, 512])
        Cbf = sb("Cbf", [128, 4, 512], BF16)
        nst = sb("nst", [128, 8])
        nbf = sb("nbf", [128, 8], BF16)
        mst = sb("mst", [4, 1])
        hmtok = [sb(f"hmtok{i}", [128, D], BF16) for i in range(2)]
        hmT = sb("hmT", [128, 8, T], BF16)
        scB = hmtok[0][:].rearrange("p (k n) -> p k n", k=8)
        wring = [sb(f"wr{i}", [128, 2048], BF16) for i in range(6)]
        g_i = sb("g_i", [4, T])
        g_lf = sb("g_lf", [4, T])
        g_nb = sb("g_nb", [4, T])
        g_t, g_w, g_d = g_lf, g_i, g_nb
        g_one = sb("g_one", [4, 128])
        g_um = sb("g_um", [4, 4])
        g_M = sb("g_M", [4, 4])
        g_dm = sb("g_dm", [4, 4])
        g_io = sb("g_io", [4, 4])
        g_iod = sb("g_iod", [4, 16])
        gcol = sb("gcol", [128, 32])
        iob = sb("iob", [128, 16])
        st6 = sb("st6", [128, 4, 6])
        mv = sb("mv", [128, 4, 2])
        sm = sb("sm", [128, 32])
        mean_sb = sb("mean_sb", [128, T])
        rstd_sb = sb("rstd_sb", [128, T])

        block = es.enter_context(nc.Block())

        def C(name, i=0, n=1):
            o = _COLS[name] + i
            return cols[:, o:o + n]

        S.dma("sp", lambda e: e.dma_start(out=cols[:], in_=cols_d), w=["cols"])
        for i, (tile_, keys_) in enumerate([(G1g, G1K), (G2g, G2K), (Gf[:], ["Gf"])]):
            S.dma("sp", lambda e: e.dma_start(out=tile_, in_=rows_d[i:i + 1, :].partition_broadcast(128)), w=keys_)
        S.dma("pool", lambda e: e.dma_start(out=bvb[:], in_=rows_d[3:4, :]), w=["bvb"])

        scr_keys = {}
        in_pieces = [(3072, 5120), (6144, 6152), (2048, 3072), (0, 2048), (5120, 6144), (6152, 8200)]

        def in_keys(c0, c1):
            ks = []
            for (a, b) in in_pieces:
                if a < c1 and c0 < b:
                    ks += [("scr", "in", a, rb) for rb in range(0, 8, 2)]
            return ks

        def in_piece_task(a, b, rb):
            return (w_in[rb * 128:(rb + 2) * 128, a:b], s_in[rb * 128:(rb + 2) * 128, a:b], ("scr", "in", a, rb))

        for (a, b) in in_pieces[:2]:
            for rb in range(0, 8, 2):
                src_, dst_, k_ = in_piece_task(a, b, rb)
                S.dma("pool", lambda e: e.dma_start(out=dst_, in_=src_), w=[k_])
        rest_list = []
        for (a, b) in in_pieces[2:]:
            for rb in range(0, 8, 2):
                rest_list.append(in_piece_task(a, b, rb))
        up_list = []
        for name, src, dst, rows_ in (("pw", w_pw, s_pw, D), ("up", w_up, s_up, D)):
            scr_keys[name] = [("scr", name, rb) for rb in range(rows_ // 128)]
            for rb in range(rows_ // 128):
                (up_list if name == "up" else rest_list).append((src[rb * 128:(rb + 1) * 128, :], dst[rb * 128:(rb + 1) * 128, :], ("scr", name, rb)))
        scr_keys["out"] = [("scr", "out", rb) for rb in range(8)]
        scr_keys["dn"] = [("scr", "dn", rb) for rb in range(22)]

        scale_tasks = [(name, src, dst, rb, Gt, gk) for name, src, dst, nrb, Gt, gk in
                       (("out", w_out, s_out, 8, G1g, G1K), ("dn", w_dn, s_dn, 22, G2g, G2K)) for rb in range(nrb)]
        diag_tasks = list(range(8))
        stg_i = [0]

        def scaled_scratch(n):
            for _ in range(n):
                if not scale_tasks:
                    break
                name, src, dst, rb, Gt, gk = scale_tasks.pop(0)
                if True:
                    for ch in range(2):
                        si = stg_i[0]
                        stg_i[0] = (si + 1) % 2
                        a32, ak = (mean_sb, "mean_sb") if si == 0 else (rstd_sb, "rstd_sb")
                        ob, ok_ = hmtok[1 - si][:, 0:512], ("hmtok", 1 - si)
                        S.dma("pool", lambda e: e.dma_start(out=a32[:], in_=src[rb * 128:(rb + 1) * 128, ch * 512:(ch + 1) * 512]), w=[ak])
                        S.pool(lambda e: e.tensor_tensor(out=ob, in0=a32[:], in1=Gt[:, ch * 512:(ch + 1) * 512], op=ALU.mult), r=[ak] + gk, w=[ok_])
                        S.dma("pool", lambda e: e.dma_start(out=dst[rb * 128:(rb + 1) * 128, ch * 512:(ch + 1) * 512], in_=ob), r=[ok_], w=[("scr", name, rb, ch)])
        gate_t = sb("gate_t", [1, 4])

        def cast_rest(st, n, xb):
            S.dve(lambda e: e.memset(gate_t[:, 0:1], 0.0), r=[("x", xb, 0)], w=[("gate", st)])
            for _ in range(n):
                if not rest_list:
                    return
                src_, dst_, k = rest_list.pop(0)
                S.dma("pool", lambda e: e.dma_start(out=dst_, in_=src_), r=[("gate", st)], w=[k])

        S.pool(lambda e: e.memset(ident[:], 0.0), w=["ident"])
        S.pool(lambda e: e.affine_select(out=ident[:], in_=ident[:], compare_op=ALU.not_equal, fill=1.0, base=0, pattern=[[-1, 128]], channel_multiplier=1), r=["ident"], w=["ident"])
        S.pool(lambda e: e.memset(maskT[:], 1.0), w=["maskT"])
        S.pool(lambda e: e.affine_select(out=maskT[:], in_=maskT[:], compare_op=ALU.is_ge, fill=0.0, base=0, pattern=[[1, 128]], channel_multiplier=-1), r=["maskT"], w=["maskT"])
        S.pool(lambda e: e.memset(onesf[:], 1.0 / D), w=["onesf"])
        S.pool(lambda e: e.memset(ones1[:], 1.0), w=["ones1"])
        S.pool(lambda e: e.memset(onesb[:], 1.0), w=["onesb"])
        S.pool(lambda e: e.memset(g_one[:], 1.0), w=["g_one"])
        S.pool(lambda e: e.memset(Cst[:], 0.0), w=[("C", h) for h in range(4)])
        S.pool(lambda e: e.memset(nst[:], 0.0), w=["nst"])
        S.pool(lambda e: e.memset(mst[:], 0.0), w=["mst"])
        S.pool(lambda e: e.memset(halo[:], 0.0), w=["halo"])
        S.pool(lambda e: e.memset(aT[:], 0.0), w=[("aT", cc) for cc in range(8)])
        S.dve(lambda e: e.tensor_copy(out=identb[:], in_=ident[:]), r=["ident"], w=["identb"])
        S.dve(lambda e: e.tensor_scalar(out=nbf_[:], in0=cols[0:4, _COLS["b_f"]:_COLS["b_f"] + 1], scalar1=-1.0, scalar2=None, op0=ALU.mult), r=["cols"], w=["nbf_"])

        tf_i = [0]

        def tmp():
            i = tf_i[0]
            tf_i[0] = (i + 1) % len(tmpf)
            return tmpf[i], ("tmpf", i)

        S.act(lambda e: e.activation(out=scc[:], in_=C("ccol", 0, 8), func=AF.Silu), r=["cols"], w=["scc"])
        for kc in range(8):
            S.dve(lambda e, kc=kc: e.tensor_scalar(out=scB[:, kc, :], in0=ones1[:], scalar1=scc[:, kc:kc + 1], scalar2=None, op0=ALU.mult), r=["ones1", "scc"], w=[("hmtok", 0)])
        wr_i = [0]

        def wslot():
            i = wr_i[0]
            wr_i[0] = (i + 1) % len(wring)
            return wring[i], ("wr", i)

        col_forms = {0: 0, 1: 1, 3: 2, 4: 3}

        def mod_vec(vi):
            for blk in range(4):
                c0 = vi * D + blk * 256
                slot, sk = wslot()
                wv = slot[:].rearrange("p (k n) -> p k n", k=8)
                S.dma("pool", lambda e: e.dma_start(out=wv, in_=w_ada[:, c0:c0 + 256].rearrange("(k p) n -> p k n", p=128)), w=[sk])
                br, kr = nb()
                for kc in range(8):
                    S.pe(lambda e: e.matmul(br[:, 0:256], lhsT=scB[:, kc, :], rhs=wv[:, kc, :], start=(kc == 0), stop=(kc == 7)), r=[sk, ("hmtok", 0)], w=[kr])
                if vi in col_forms:
                    mk = "modr1" if vi < 2 else "modr2"
                    for sub in range(2):
                        cidx = col_forms[vi] * 8 + blk * 2 + sub
                        t_, tk = tmp()
                        S.dve(lambda e: e.tensor_tensor(out=t_[:, 0:128], in0=br[:, sub * 128:(sub + 1) * 128], in1=ident[:], op=ALU.mult), r=["ident"], w=[tk, kr])
                        S.dve(lambda e: e.tensor_reduce(out=modr[:, cidx:cidx + 1], in_=t_[:, 0:128], axis=AX.X, op=ALU.add), r=[tk], w=[mk])
                else:
                    Gt, gk = (G1g, G1K) if vi == 2 else (G2g, G2K)
                    S.dve(lambda e: e.tensor_tensor(out=Gt[:, blk * 256:(blk + 1) * 256], in0=br[:, 0:256], in1=Gt[:, blk * 256:(blk + 1) * 256], op=ALU.add), r=gk, w=gk + [kr])

        def mod_part1():
            mod_vec(1)
            mod_vec(0)
            S.dve(lambda e: e.tensor_tensor(out=modc[:, 0:16], in0=modr[:, 0:16], in1=C("bada", 0, 16), op=ALU.add), r=["cols", "modr1"], w=["modc1"])
            S.dve(lambda e: e.scalar_tensor_tensor(out=G1c[:], in0=modc[:, 8:16], scalar=1.0, in1=C("nmg", 0, 8), op0=ALU.add, op1=ALU.mult), r=["modc1", "cols"], w=["G1c"])

        def mod_part2():
            mod_vec(2)
            mod_vec(4)
            mod_vec(3)
            S.dve(lambda e: e.tensor_tensor(out=modc[:, 16:32], in0=modr[:, 16:32], in1=C("bada", 16, 16), op=ALU.add), r=["cols", "modr2"], w=["modc2"])
            S.dve(lambda e: e.scalar_tensor_tensor(out=G2c[:], in0=modc[:, 24:32], scalar=1.0, in1=C("nfg", 0, 8), op0=ALU.add, op1=ALU.mult), r=["modc2", "cols"], w=["G2c"])
            mod_vec(5)


        def wload(src, keys, r0, nr, c0, ncol):
            slot, sk = wslot()
            if keys is None:
                keys = in_keys(c0, c0 + ncol)
            v = slot[:, 0:nr * ncol].rearrange("p (k n) -> p k n", k=nr)
            S.dma("sp", lambda e: e.dma_start(out=v, in_=src[r0:r0 + nr * 128, c0:c0 + ncol].rearrange("(k p) n -> p k n", p=128)), r=keys, w=[sk])
            return v, sk

        stats_done = {}
        ffn_stats_done = [False]

        def rms_stats(xb, si):
            xt = xts[xb]
            ssq_, sk_ = ssqs[si], ("ssq", si)
            rstd_, rk_ = rstds[si], ("rstd", si)
            for j in range(4):
                S.act(lambda e: e.activation(out=sqj[:], in_=xt[:, j, :], func=AF.Square, accum_out=ssq_[:, j:j + 1]), r=[("x", xb, j)], w=["sqj", sk_])
            S.act(lambda e: e.activation(out=rstd_[:], in_=ssq_[:], func=AF.Sqrt, scale=1.0 / D, bias=EPS), r=[sk_], w=[rk_])
            S.dve(lambda e: e.reciprocal(out=rstd_[:], in_=rstd_[:]), r=[rk_], w=[rk_])
            if si == 0:
                stats_done[xb] = True

        def rmsnorm_T(Gc, shc_off, xb, dst=None):
            xt = xts[xb]
            if dst is None:
                dst, dkey = hT, (lambda fc: [("hT", fc)])
            else:
                dkey = lambda fc: [("hmT", j) for j in range(4)]
            mkey, gkey = ("modc1", "G1c") if shc_off == 0 else ("modc2", "G2c")
            si = 0 if shc_off == 0 else 1
            rstd, rk = rstds[si], ("rstd", si)
            if si == 1 and ffn_stats_done[0]:
                ffn_stats_done[0] = False
            elif not (si == 0 and stats_done.get(xb)):
                rms_stats(xb, si)
            if si == 0:
                stats_done[xb] = False
            bks = [nb(hold=True) for _ in range(4)]
            for j in range(4):
                xs, xsk = xsb[j % 2], ("xs", j % 2)
                S.act(lambda e: e.activation(out=xs[:], in_=xt[:, j, :], func=AF.Copy, scale=rstd[:, j:j + 1]), r=[("x", xb, j), rk], w=[xsk])
                for fc in range(8):
                    b, bk = bks[fc // 2]
                    col = ((fc % 2) * 4 + j) * 128
                    S.pe(lambda e: e.transpose(out=b[:].bitcast(BF16)[:, col:col + 128], in_=xs[:, fc * 128:(fc + 1) * 128], identity=identb[:]), r=[xsk, "identb"], w=[bk])
            for fc in range(8):
                b, bk = bks[fc // 2]
                src = b[:].bitcast(BF16)[:, (fc % 2) * 512:(fc % 2 + 1) * 512]
                if (fc // 2) % 2 == 0:
                    S.act(lambda e: e.activation(out=dst[:, fc, :], in_=src, func=AF.Identity, scale=Gc[:, fc:fc + 1], bias=modc[:, shc_off + fc:shc_off + fc + 1]), r=[mkey, gkey], w=dkey(fc) + [bk])
                else:
                    S.dve(lambda e: e.tensor_scalar(out=dst[:, fc, :], in0=src, scalar1=Gc[:, fc:fc + 1], scalar2=modc[:, shc_off + fc:shc_off + fc + 1], op0=ALU.mult, op1=ALU.add), r=[mkey, gkey], w=dkey(fc) + [bk])
            rel(*[k for _, k in bks])

        hT_all = [("hT", k) for k in range(8)]

        def projA(wv, wk, ml, rhs_keys=hT_all, rhs=None):
            b, bk = nb()
            for kc in range(8):
                S.pe(lambda e, kc=kc: e.matmul(b[:], lhsT=wv[:, kc, ml * 128:(ml + 1) * 128], rhs=(hT if rhs is None else rhs)[:, kc, :], start=(kc == 0), stop=(kc == 7)), r=[wk] + list(rhs_keys), w=[bk])
            return b, bk

        IN_K = None

        dg_i = [0]
        dq_i = [0]
        qb_i = [0]

        def dgload(cc, hf):
            i = dg_i[0]
            dg_i[0] = (i + 1) % 2
            S.dma("sp", lambda e: e.dma_start(out=dgbuf[i][:], in_=s_dg[2 * cc + hf]), r=[("sdg", 2 * cc + hf)], w=[("dgbuf", i)])
            return dgbuf[i], ("dgbuf", i)

        def dqload(p):
            i = dq_i[0]
            dq_i[0] = (i + 1) % 2
            S.dma("sp", lambda e: e.dma_start(out=dqbuf[i][:], in_=s_dq[p]), r=[("sdq", p)], w=[("dqbuf", i)])
            return dqbuf[i], ("dqbuf", i)

        def make_diag_scratch_qk():
            for p in range(8):
                i = dq_i[0]
                dq_i[0] = (i + 1) % 2
                for ml in range(2):
                    for j in range(4):
                        o = (ml * 4 + j) * 128
                        S.dve(lambda e: e.tensor_scalar(out=dqbuf[i][:, o:o + 128], in0=identb[:], scalar1=C("qkw", (2 * p + ml) * 4 + j), scalar2=None, op0=ALU.mult), r=["identb", "cols"], w=[("dqbuf", i)])
                S.dma("sp", lambda e: e.dma_start(out=s_dq[p], in_=dqbuf[i][:]), r=[("dqbuf", i)], w=[("sdq", p)])

        def make_diag_scratch_conv(n):
            for _ in range(n):
                if not diag_tasks:
                    break
                cc = diag_tasks.pop(0)
                for hf in range(2):
                    i = dg_i[0]
                    dg_i[0] = (i + 1) % 2
                    for jl in range(16 if hf == 0 else 15):
                        j = hf * 16 + jl
                        S.dve(lambda e: e.tensor_scalar(out=dgbuf[i][:, jl * 128:(jl + 1) * 128], in0=identb[:], scalar1=C("convw", cc * 31 + j), scalar2=None, op0=ALU.mult), r=["identb", "cols"], w=[("dgbuf", i)])
                    if hf == 1:
                        S.dve(lambda e: e.memset(dgbuf[i][:, 15 * 128:16 * 128], 0.0), w=[("dgbuf", i)])
                    S.dma("sp", lambda e: e.dma_start(out=s_dg[2 * cc + hf], in_=dgbuf[i][:]), r=[("dgbuf", i)], w=[("sdg", 2 * cc + hf)])

        cvs = {}
        NPOOL = 0
        NDVE = 0

        def conv_begin():
            cvs["s1"] = nb(hold=True)
            cvs["s2"] = nb(hold=True)

        def conv_stage(cc):
            (bs1, ks1), (bs2, ks2) = cvs["s1"], cvs["s2"]
            bc, bck = nb()
            for hf in range(2):
                taps = [hf * 16 + jl for jl in range(16 if hf == 0 else 15) if hf * 16 + jl >= NPOOL + NDVE]
                if not taps:
                    continue
                dg_, dgk = dgload(cc, hf)
                for j in taps:
                    jl = j - hf * 16
                    S.pe(lambda e: e.matmul(bc[:], lhsT=dg_[:, jl * 128:(jl + 1) * 128], rhs=aT[:, cc, j:j + T], start=(j == NPOOL + NDVE), stop=(j == 30)), r=[dgk, ("aT", cc)], w=[bck])
            yc = big[:, cc * 512:(cc + 1) * 512]
            yk = [("big", 2 * cc), ("big", 2 * cc + 1)]
            sq, sqk = tmp()
            S.act(lambda e: e.activation(out=yc, in_=bc[:], func=AF.Identity, bias=C("convb", cc), scale=1.0), r=["cols"], w=yk + [bck])
            S.act(lambda e: e.activation(out=sq[:], in_=bc[:], func=AF.Square, bias=C("convb", cc), scale=1.0), r=["cols"], w=[sqk, bck])
            S.pe(lambda e: e.matmul(bs1[:], lhsT=onesf[:], rhs=yc, start=(cc == 0), stop=(cc == 7)), r=yk + ["onesf"], w=[ks1])
            S.pe(lambda e: e.matmul(bs2[:], lhsT=onesf[:], rhs=sq[:], start=(cc == 0), stop=(cc == 7)), r=[sqk, "onesf"], w=[ks2])
            S.pool(lambda e: e.tensor_copy(out=aT[:, cc, 0:30], in_=aT[:, cc, T:T + 30]), r=[("aT", cc)], w=[("aT", cc)])

        def conv_finish():
            (bs1, ks1), (bs2, ks2) = cvs["s1"], cvs["s2"]
            nm, nmk = tmp()
            S.act(lambda e: e.activation(out=mean_sb[:], in_=bs1[:], func=AF.Copy), w=["mean_sb", ks1])
            S.dve(lambda e: e.scalar_tensor_tensor(out=nm[:], in0=mean_sb[:], scalar=-1.0, in1=mean_sb[:], op0=ALU.mult, op1=ALU.mult), r=["mean_sb"], w=[nmk])
            S.dve(lambda e: e.tensor_tensor(out=nm[:], in0=bs2[:], in1=nm[:], op=ALU.add), r=[nmk], w=[nmk, ks2])
            rel(ks1, ks2)
            S.act(lambda e: e.activation(out=rstd_sb[:], in_=nm[:], func=AF.Sqrt, bias=EPS, scale=1.0), r=[nmk], w=["rstd_sb"])
            S.dve(lambda e: e.reciprocal(out=rstd_sb[:], in_=rstd_sb[:]), r=["rstd_sb"], w=["rstd_sb"])
            for cc in range(8):
                yc = big[:, cc * 512:(cc + 1) * 512]
                yk = [("big", 2 * cc), ("big", 2 * cc + 1)]
                S.dve(lambda e: e.tensor_tensor(out=yc, in0=yc, in1=mean_sb[:], op=ALU.subtract), r=yk + ["mean_sb"], w=yk)
                S.dve(lambda e: e.tensor_tensor(out=yc, in0=yc, in1=rstd_sb[:], op=ALU.mult), r=yk + ["rstd_sb"], w=yk)
                S.act(lambda e: e.activation(out=bigb[:, 8192 + cc * 512:8192 + (cc + 1) * 512], in_=yc, func=AF.Silu, scale=C("lng", cc), bias=C("lnb", cc)), r=yk + ["cols"], w=[("big", 16 + cc)])

        def glu_part(do_conv):
            for g in range(4):
                wa, wak = wload(s_in, IN_K, 0, 8, g * 256, 256)
                wgt, wgk = wload(s_in, IN_K, 0, 8, 1024 + g * 256, 256)
                for ml in range(2):
                    cc = 2 * g + ml
                    ba, bak = projA(wa, wak, ml)
                    bg, bgk = projA(wgt, wgk, ml)
                    sg, sgk = tmp()
                    S.act(lambda e: e.activation(out=sg[:], in_=bg[:], func=AF.Sigmoid, bias=C("b_glu", 8 + cc), scale=1.0), r=["cols"], w=[sgk, bgk])
                    S.dve(lambda e: e.scalar_tensor_tensor(out=aT[:, cc, 30:30 + T], in0=ba[:], scalar=C("b_glu", cc), in1=sg[:], op0=ALU.add, op1=ALU.mult), r=[sgk, "cols"], w=[("aT", cc), bak])
                    if not do_conv:
                        S.pool(lambda e: e.tensor_copy(out=aT[:, cc, 0:30], in_=aT[:, cc, T:T + 30]), r=[("aT", cc)], w=[("aT", cc)])

        kres = bigb[:, 4096:12288].rearrange("p (k n) -> p k n", k=8)
        KRES_K = [("big", i) for i in range(8, 24)]

        def qk_part(mcs, kb=8, res=False):
            def conv_stage(p):
                mc, qb, qbk, dq, dqk, ml = p
                oi = (kb + mc - 8) if mc >= 8 else mc
                b2, b2k = nb()
                for j in range(4):
                    S.pe(lambda e: e.matmul(b2[:], lhsT=dq[:, (ml * 4 + j) * 128:(ml * 4 + j + 1) * 128], rhs=qb[:, j:j + T], start=(j == 0), stop=(j == 3)), r=[dqk, qbk], w=[b2k])
                S.act(lambda e: e.activation(out=halo[:, mc * 3:mc * 3 + 3], in_=qb[:, T:T + 3], func=AF.Copy), r=[qbk, "halo"], w=["halo"])
                S.act(lambda e: e.activation(out=qkT[:, oi, :], in_=b2[:], func=AF.Silu, bias=C("qkb", mc), scale=1.0), r=["cols"], w=[("qkT", oi), b2k])

            pend = None
            for g0 in range(0, len(mcs), 2):
                mc0 = mcs[g0]
                if not res:
                    wv, wk = wload(s_in, IN_K, 0, 8, 2048 + mc0 * 128, 256)
                dq, dqk = dqload(mc0 // 2)
                for ml in range(2):
                    mc = mc0 + ml
                    if res:
                        b, bk = nb()
                        for kc in range(8):
                            S.pe(lambda e: e.matmul(b[:], lhsT=kres[:, kc, (mc - 8) * 128:(mc - 7) * 128], rhs=hT[:, kc, :], start=(kc == 0), stop=(kc == 7)), r=KRES_K + hT_all, w=[bk])
                    else:
                        b, bk = projA(wv, wk, ml)
                    r_ = qb_i[0]
                    qb_i[0] = (r_ + 1) % 3
                    qb, qbk = qkb16[r_], ("qkb16", r_)
                    S.act(lambda e: e.activation(out=qb[:, 3:3 + T], in_=b[:], func=AF.Identity, bias=C("b_qk", mc), scale=1.0), r=["cols"], w=[qbk, bk])
                    S.act(lambda e: e.activation(out=qb[:, 0:3], in_=halo[:, mc * 3:mc * 3 + 3], func=AF.Copy), r=["halo", qbk], w=[qbk])
                    if pend is not None:
                        conv_stage(pend)
                    pend = (mc, qb, qbk, dq, dqk, ml)
            conv_stage(pend)

        wg_done = [False]

        def gates_part():
            if not wg_done[0]:
                wg_done[0] = True
                S.dma("sp", lambda e: e.dma_start(out=wg[:], in_=s_in[:, 6144:6152].rearrange("(k p) n -> p k n", p=128)), r=in_keys(6144, 6152), w=["wg"])
            bi, bik = nb()
            bf_, bfk = nb()
            for kc in range(8):
                S.pe(lambda e, kc=kc: e.matmul(bi[0:4, :], lhsT=wg[:, kc, 0:4], rhs=hT[:, kc, :], start=(kc == 0), stop=(kc == 7)), r=["wg"] + hT_all, w=[bik])
            for kc in range(8):
                S.pe(lambda e, kc=kc: e.matmul(bf_[0:4, :], lhsT=wg[:, kc, 4:8], rhs=hT[:, kc, :], start=(kc == 0), stop=(kc == 7)), r=["wg"] + hT_all, w=[bfk])
            S.act(lambda e: e.activation(out=g_i[:], in_=bi[0:4, :], func=AF.Identity, bias=cols[0:4, _COLS["b_i"]:_COLS["b_i"] + 1], scale=1.0), r=["cols"], w=["g_i", bik])
            S.act(lambda e: e.activation(out=g_lf[:], in_=bf_[0:4, :], func=AF.Exp, bias=nbf_[:], scale=-1.0), r=["nbf_"], w=["g_lf", bfk])
            S.act(lambda e: e.activation(out=g_lf[:], in_=g_lf[:], func=AF.Ln, bias=1.0, scale=1.0), r=["g_lf"], w=["g_lf"])
            for j in range(4):
                S.dve(lambda e, j=j: e.tensor_tensor_scan(out=g_nb[:, j * 128:(j + 1) * 128], data0=g_one[:], data1=g_lf[:, j * 128:(j + 1) * 128], initial=0.0, op0=ALU.mult, op1=ALU.add), r=["g_one", "g_lf"], w=["g_nb"])
            S.dve(lambda e: e.tensor_tensor(out=g_i[:], in0=g_i[:], in1=g_nb[:], op=ALU.add), r=["g_i", "g_nb"], w=["g_i"])
            S.dve(lambda e: e.tensor_reduce(out=g_um[:], in_=g_i[:].rearrange("p (j t) -> p j t", j=4), axis=AX.X, op=ALU.max), r=["g_i"], w=["g_um"])
            for j in range(4):
                S.dve(lambda e, j=j: e.tensor_tensor(out=g_M[:, j:j + 1], in0=mst[:], in1=g_um[:, j:j + 1], op=ALU.max), r=["mst", "g_um"], w=["g_M"])
                S.dve(lambda e, j=j: e.tensor_tensor(out=g_dm[:, j:j + 1], in0=mst[:], in1=g_M[:, j:j + 1], op=ALU.subtract), r=["mst", "g_M"], w=["g_dm"])
                S.dve(lambda e, j=j: e.tensor_tensor(out=mst[:], in0=g_M[:, j:j + 1], in1=g_nb[:, j * 128 + 127:j * 128 + 128], op=ALU.subtract), r=["g_M", "g_nb"], w=["mst"])
            S.act(lambda e: e.activation(out=g_io[:], in_=g_dm[:], func=AF.Exp), r=["g_dm"], w=["g_io"])
            for j in range(4):
                S.dve(lambda e, j=j: e.tensor_scalar(out=g_t[:, j * 128:(j + 1) * 128], in0=g_i[:, j * 128:(j + 1) * 128], scalar1=g_M[:, j:j + 1], scalar2=-LN16, op0=ALU.subtract, op1=ALU.add), r=["g_i", "g_M"], w=["g_lf"])
            S.act(lambda e: e.activation(out=g_w[:], in_=g_t[:], func=AF.Exp), r=["g_lf"], w=["g_i"])
            for j in range(4):
                S.dve(lambda e, j=j: e.tensor_scalar(out=g_t[:, j * 128:(j + 1) * 128], in0=g_nb[:, j * 128:(j + 1) * 128], scalar1=g_M[:, j:j + 1], scalar2=None, op0=ALU.subtract), r=["g_nb", "g_M", "g_i"], w=["g_lf"])
            S.act(lambda e: e.activation(out=g_d[:], in_=g_t[:], func=AF.Exp), r=["g_lf"], w=["g_nb"])
            for h in range(4):
                S.dve(lambda e, h=h: e.tensor_scalar(out=g_iod[:].rearrange("p (j h) -> p j h", h=4)[:, :, h], in0=g_io[:], scalar1=ident[0:4, h:h + 1], scalar2=None, op0=ALU.mult), r=["g_io", "ident"], w=["g_iod"])
        def gates_B():
            bt, btk = nb()
            for j in range(4):
                S.pe(lambda e, j=j: e.transpose(out=bt[:, j * 8:j * 8 + 4], in_=g_w[:, j * 128:(j + 1) * 128], identity=ident[0:4, 0:4]), r=["g_i", "ident"], w=[btk])
                S.pe(lambda e, j=j: e.transpose(out=bt[:, j * 8 + 4:j * 8 + 8], in_=g_d[:, j * 128:(j + 1) * 128], identity=ident[0:4, 0:4]), r=["g_nb", "ident"], w=[btk])
            S.pe(lambda e: e.matmul(bt[:, 32:48], lhsT=ones1[0:4, :], rhs=g_iod[:], start=True, stop=True), r=["ones1", "g_iod"], w=[btk])
            S.dve(lambda e: e.tensor_copy(out=gcol[:], in_=bt[:, 0:32]), w=["gcol", btk])
            S.dve(lambda e: e.tensor_copy(out=iob[:], in_=bt[:, 32:48]), w=["iob", btk])

        vkey_main = lambda j: [("v", j)]
        vkey_alt = lambda j: [("hmT", jj) for jj in range(4)]
        vtok_alt = hmT[:].rearrange("p c t -> p (c t)").rearrange("p (j d) -> p j d", j=4)

        def v_part(vdst=None, vkeyf=None):
            if vdst is None:
                vdst, vkeyf = vtok, vkey_main
            for h in range(4):
                wv, wk = wload(s_in, IN_K, 0, 8, 4096 + h * 256, 256)
                for j in range(4):
                    b, bk = nb()
                    S.pe(lambda e, b=b, h=h: e.matmul(b[:, 0:256], lhsT=onesb[0:1, :], rhs=bvb[0:1, h * 256:(h + 1) * 256], start=True, stop=False), r=["onesb", "bvb"], w=[bk])
                    for kc in range(8):
                        S.pe(lambda e, b=b, kc=kc, j=j, wv=wv: e.matmul(b[:, 0:256], lhsT=hT[:, kc, j * 128:(j + 1) * 128], rhs=wv[:, kc, :], start=False, stop=(kc == 7)), r=[wk] + hT_all, w=[bk])
                    if (h + j) % 2 == 0:
                        S.act(lambda e, b=b, j=j, h=h: e.activation(out=vdst[:, j, h * 256:(h + 1) * 256], in_=b[:, 0:256], func=AF.Copy), w=vkeyf(j) + [bk])
                    else:
                        S.dve(lambda e, b=b, j=j, h=h: e.tensor_copy(out=vdst[:, j, h * 256:(h + 1) * 256], in_=b[:, 0:256]), w=vkeyf(j) + [bk])

        kt_i = [0]
        hm_i = [0]

        def ml_A(j, full, kb=8):
            cs = slice(j * 128, (j + 1) * 128)
            ctx = {"j": j, "cs": cs}
            bt, btk = nb()
            btb = bt[:].bitcast(BF16)
            for mc in range(8):
                S.pe(lambda e: e.transpose(out=btb[:, mc * 128:(mc + 1) * 128], in_=qkT[:, kb + mc, cs], identity=identb[:]), r=[("qkT", kb + mc), "identb"], w=[btk])
            ki = kt_i[0]
            kt_i[0] = (ki + 1) % 2
            kt, ktk = ktok[ki], ("ktok", ki)
            ctx["kt"], ctx["ktk"] = kt, ktk
            for h in range(4):
                S.act(lambda e: e.activation(out=kt[:, h * 256:(h + 1) * 256], in_=btb[:, h * 256:(h + 1) * 256], func=AF.Copy, scale=gcol[:, j * 8 + h:j * 8 + h + 1]), r=["gcol"], w=[ktk, btk])
            if full:
                bsc, bsck = nb()
                for h in range(4):
                    for dc in range(2):
                        S.pe(lambda e: e.matmul(bsc[:, h * 128:(h + 1) * 128], lhsT=qkT[:, 8 + 2 * h + dc, cs], rhs=qkT[:, 2 * h + dc, cs], start=(dc == 0), stop=(dc == 1)), r=[("qkT", 8 + 2 * h + dc), ("qkT", 2 * h + dc)], w=[bsck])
                scd, scdk = tmp()
                scdb = scd[:].bitcast(BF16)
                ctx["scdb"], ctx["scdk"] = scdb, scdk
                for h in range(4):
                    S.dve(lambda e: e.scalar_tensor_tensor(out=scdb[:, h * 128:(h + 1) * 128], in0=bsc[:, h * 128:(h + 1) * 128], scalar=gcol[:, j * 8 + h:j * 8 + h + 1], in1=maskT[:], op0=ALU.mult, op1=ALU.mult), r=["gcol", "maskT"], w=[scdk, bsck])
            return ctx

        def ml_B(ctx, full, vsrc=None, vkeyf=None):
            if vsrc is None:
                vsrc, vkeyf = vtok, vkey_main
            j, cs, kt, ktk = ctx["j"], ctx["cs"], ctx["kt"], ctx["ktk"]
            if full:
                scdb, scdk = ctx["scdb"], ctx["scdk"]
                for h in range(4):
                    S.act(lambda e: e.activation(out=Cbf[:, h, :], in_=Cst[:, h, :], func=AF.Copy, scale=iob[:, j * 4 + h:j * 4 + h + 1]), r=[("C", h), "iob"], w=[("Cbf", h)])
                    S.dve(lambda e: e.tensor_scalar(out=nbf[:, 2 * h:2 * h + 2], in0=nst[:, 2 * h:2 * h + 2], scalar1=iob[:, j * 4 + h:j * 4 + h + 1], scalar2=None, op0=ALU.mult), r=["nst", "iob"], w=["nbf"])
                bnA, bnAk = nb(hold=True)
                bnB, bnBk = nb(hold=True)
                ctx["bn"] = (bnA, bnAk, bnB, bnBk)
            bq, bqk = nb(hold=True)
            if full:
                for h in range(4):
                    bn, bnk = (bnA, bnAk) if h < 2 else (bnB, bnBk)
                    osl = slice((h % 2) * 256, (h % 2 + 1) * 256)
                    S.pe(lambda e: e.matmul(bn[:, osl], lhsT=scdb[:, h * 128:(h + 1) * 128], rhs=vtok[:, j, h * 256:(h + 1) * 256], start=True, stop=False), r=[scdk, ("v", j)], w=[bnk])
                    for dc in range(2):
                        S.pe(lambda e: e.matmul(bn[:, osl], lhsT=qkT[:, 2 * h + dc, cs], rhs=Cbf[:, h, dc * 256:(dc + 1) * 256], start=False, stop=(dc == 1)), r=[("qkT", 2 * h + dc), ("Cbf", h)], w=[bnk])
                    S.pe(lambda e: e.matmul(bq[:, h:h + 1], lhsT=scdb[:, h * 128:(h + 1) * 128], rhs=onesb[:, 0:1], start=True, stop=False), r=[scdk, "onesb"], w=[bqk])
                    for dc in range(2):
                        S.pe(lambda e: e.matmul(bq[:, h:h + 1], lhsT=qkT[:, 2 * h + dc, cs], rhs=nbf[:, 2 * h + dc:2 * h + dc + 1], start=False, stop=(dc == 1)), r=[("qkT", 2 * h + dc), "nbf"], w=[bqk])
            for h in range(4):
                bd, bdk = nb()
                for dc in range(2):
                    S.pe(lambda e: e.matmul(bd[:, dc * 256:(dc + 1) * 256], lhsT=kt[:, h * 256 + dc * 128:h * 256 + (dc + 1) * 128], rhs=vsrc[:, j, h * 256:(h + 1) * 256], start=True, stop=True), r=[ktk] + vkeyf(j), w=[bdk])
                    S.pe(lambda e: e.matmul(bq[:, 8 + 2 * h + dc:8 + 2 * h + dc + 1], lhsT=kt[:, h * 256 + dc * 128:h * 256 + (dc + 1) * 128], rhs=onesb[:, 0:1], start=True, stop=True), r=[ktk, "onesb"], w=[bqk])
                S.dve(lambda e: e.scalar_tensor_tensor(out=Cst[:, h, :], in0=Cst[:, h, :], scalar=iob[:, j * 4 + h:j * 4 + h + 1], in1=bd[:], op0=ALU.mult, op1=ALU.add), r=[("C", h), "iob"], w=[("C", h), bdk])
            if full:
                S.dve(lambda e: e.tensor_copy(out=sm[:, 0:4], in_=bq[:, 0:4]), w=["sm", bqk])
            for h in range(4):
                S.dve(lambda e: e.scalar_tensor_tensor(out=nst[:, 2 * h:2 * h + 2], in0=nst[:, 2 * h:2 * h + 2], scalar=iob[:, j * 4 + h:j * 4 + h + 1], in1=bq[:, 8 + 2 * h:8 + 2 * h + 2], op0=ALU.mult, op1=ALU.add), r=["nst", "iob"], w=["nst", bqk])
            rel(bqk)

        def ml_Cpre(ctx):
            j = ctx["j"]
            bnA, bnAk, bnB, bnBk = ctx["bn"]
            for h in range(4):
                bn, bnk = (bnA, bnAk) if h < 2 else (bnB, bnBk)
                osl = slice((h % 2) * 256, (h % 2 + 1) * 256)
                S.dve(lambda e: e.bn_stats(out=st6[:, h, :], in_=bn[:, osl]), w=["st6", bnk])
            for h in range(4):
                S.dve(lambda e: e.bn_aggr(out=mv[:, h, :], in_=st6[:, h, :]), r=["st6"], w=["mv"])
            S.dve(lambda e: e.tensor_scalar(out=sm[:, 4:8], in0=sm[:, 0:4], scalar1=-1.0, scalar2=None, op0=ALU.mult), r=["sm"], w=["sm"])
            S.dve(lambda e: e.tensor_tensor(out=sm[:, 4:8], in0=sm[:, 4:8], in1=sm[:, 0:4], op=ALU.max), r=["sm"], w=["sm"])
            S.dve(lambda e: e.tensor_tensor(out=sm[:, 8:12], in0=sm[:, 4:8], in1=gcol[:, j * 8 + 4:j * 8 + 8], op=ALU.max), r=["sm", "gcol"], w=["sm"])
            S.dve(lambda e: e.tensor_tensor(out=sm[:, 12:16], in0=sm[:, 8:12], in1=sm[:, 8:12], op=ALU.mult), r=["sm"], w=["sm"])
            S.dve(lambda e: e.scalar_tensor_tensor(out=sm[:, 16:20], in0=sm[:, 12:16], scalar=EPS, in1=mv[:, :, 1], op0=ALU.mult, op1=ALU.add), r=["sm", "mv"], w=["sm"])
            S.act(lambda e: e.activation(out=sm[:, 20:24], in_=sm[:, 16:20], func=AF.Sqrt), r=["sm"], w=["sm"])
            S.dve(lambda e: e.reciprocal(out=sm[:, 24:28], in_=sm[:, 20:24]), r=["sm"], w=["sm"])
            S.dve(lambda e: e.scalar_tensor_tensor(out=sm[:, 28:32], in0=mv[:, :, 0], scalar=-1.0, in1=sm[:, 24:28], op0=ALU.mult, op1=ALU.mult), r=["sm", "mv"], w=["sm"])
            hi = hm_i[0]
            hm_i[0] = (hi + 1) % 2
            hm, hmk = hmtok[hi], ("hmtok", hi)
            ctx["hm"], ctx["hmk"] = hm, hmk
            for h in range(4):
                bn, bnk = (bnA, bnAk) if h < 2 else (bnB, bnBk)
                osl = slice((h % 2) * 256, (h % 2 + 1) * 256)
                S.act(lambda e: e.activation(out=hm[:, h * 256:(h + 1) * 256], in_=bn[:, osl], func=AF.Identity, scale=sm[:, 24 + h:25 + h], bias=sm[:, 28 + h:29 + h]), r=["sm"], w=[hmk, bnk])
            rel(bnAk, bnBk)

        def ml_Ctr(ctx):
            j, cs, hm, hmk = ctx["j"], ctx["cs"], ctx["hm"], ctx["hmk"]
            bh, bhk = nb()
            bhb = bh[:].bitcast(BF16)
            for cc in range(8):
                S.pe(lambda e: e.transpose(out=bhb[:, cc * 128:(cc + 1) * 128], in_=hm[:, cc * 128:(cc + 1) * 128], identity=identb[:]), r=[hmk, "identb"], w=[bhk])
            S.dve(lambda e: e.tensor_copy(out=hmT[:, :, cs], in_=bhb.rearrange("p (c t) -> p c t", c=8)), w=[("hmT", j), bhk])

        def mlstm_all(full, kb=8, vsrc=None, vkeyf=None):
            ctxs = [None] * 4
            if full:
                conv_begin()
            ctxs[0] = ml_A(0, full, kb)
            for j in range(4):
                if j < 3:
                    ctxs[j + 1] = ml_A(j + 1, full, kb)
                ml_B(ctxs[j], full, vsrc, vkeyf)
                if full:
                    conv_stage(2 * j)
                    ml_Cpre(ctxs[j])
                    conv_stage(2 * j + 1)
                    if j > 0:
                        ml_Ctr(ctxs[j - 1])
            if full:
                ml_Ctr(ctxs[3])
                conv_finish()

        def merge_part():
            hmT_all = [("hmT", j) for j in range(4)]
            a2v = bigb[:, 8192:12288].rearrange("p (k t) -> p k t", k=8)
            a2k = [("big", 16 + k) for k in range(8)]
            for g in range(4):
                wo, wok = wload(s_in, IN_K, 0, 8, 5120 + g * 256, 256)
                wc, wck = wload(s_in, IN_K, 0, 8, 6152 + g * 256, 256)
                wm, wmk = wload(s_in, IN_K, 0, 8, 7176 + g * 256, 256)
                wp, wpk = wload(s_pw, scr_keys["pw"], 0, 8, g * 256, 256)
                st_ = []
                for ml in range(2):
                    cc = 2 * g + ml
                    bo, bok = projA(wo, wok, ml)
                    bm, bmk = projA(wm, wmk, ml)
                    so, sok = tmp()
                    S.act(lambda e: e.activation(out=so[:], in_=bo[:], func=AF.Sigmoid, bias=C("b_o", cc), scale=1.0), r=["cols"], w=[sok, bok])
                    S.act(lambda e: e.activation(out=ptmp[:], in_=bm[:], func=AF.Sigmoid, bias=C("b_mm", cc), scale=1.0), r=["cols"], w=["ptmp", bmk])
                    S.dve(lambda e: e.tensor_tensor(out=so[:], in0=so[:], in1=ptmp[:], op=ALU.mult), r=[sok, "ptmp"], w=[sok])
                    S.dve(lambda e: e.scalar_tensor_tensor(out=so[:], in0=hmT[:, cc, :], scalar=C("mng", cc), in1=so[:], op0=ALU.mult, op1=ALU.mult), r=[sok, "cols"] + hmT_all, w=[sok])
                    bc, bck = projA(wc, wck, ml)
                    sgc, sgck = tmp()
                    S.act(lambda e: e.activation(out=sgc[:], in_=bc[:], func=AF.Sigmoid, bias=C("b_mc", cc), scale=1.0), r=["cols"], w=[sgck, bck])
                    st_.append((cc, ml, so, sok, sgc, sgck))
                for cc, ml, so, sok, sgc, sgck in st_:
                    bp, bpk = projA(wp, wpk, ml, rhs_keys=a2k, rhs=a2v)
                    S.dve(lambda e: e.scalar_tensor_tensor(out=sgc[:], in0=bp[:], scalar=C("b_pw", cc), in1=sgc[:], op0=ALU.add, op1=ALU.mult), r=[sgck, "cols"], w=[sgck, bpk])
                    S.dve(lambda e: e.tensor_tensor(out=qkT[:, 8 + cc, :], in0=sgc[:], in1=so[:], op=ALU.add), r=[sgck, sok], w=[("qkT", 8 + cc)])

        def resid_proj(kind, xb, hook=None):
            xt = xts[xb]
            for cb in range(2):
                bj = [nb(hold=True) for _ in range(4)]
                if kind == "out":
                    for half in range(2):
                        wv, wk = wload(s_out, [("scr", "out", rb, cb) for rb in range(half * 4, half * 4 + 4)], half * 512, 4, cb * 512, 512)
                        for kl in range(4):
                            kc = half * 4 + kl
                            for j in range(4):
                                b, bk = bj[j]
                                S.pe(lambda e: e.matmul(b[:], lhsT=qkT[:, 8 + kc, j * 128:(j + 1) * 128], rhs=wv[:, kl, :], start=(kc == 0), stop=(kc == 7)), r=[wk, ("qkT", 8 + kc)], w=[bk])
                else:
                    for fp in range(11):
                        wv, wk = wload(s_dn, [("scr", "dn", rb, cb) for rb in (2 * fp, 2 * fp + 1)], fp * 256, 2, cb * 512, 512)
                        for fl in range(2):
                            f = 2 * fp + fl
                            for j in range(4):
                                b, bk = bj[j]
                                S.pe(lambda e: e.matmul(b[:], lhsT=bigb[:, f * 512 + j * 128:f * 512 + (j + 1) * 128], rhs=wv[:, fl, :], start=(f == 0), stop=(f == 21)), r=[wk, ("big", f)], w=[bk])
                if cb == 1 and hook is not None:
                    hook()
                for j in range(4):
                    b, bk = bj[j]
                    S.dve(lambda e: e.tensor_tensor(out=xt[:, j, cb * 512:(cb + 1) * 512], in0=b[:], in1=xt[:, j, cb * 512:(cb + 1) * 512], op=ALU.add), r=[("x", xb, j)], w=[("x", xb, j), bk])
                rel(*[k for _, k in bj])

        def ffn_up():
            for fp in range(11):
                wgv, wgk = wload(s_up, scr_keys["up"], 0, 8, fp * 256, 256)
                wuv, wuk = wload(s_up, scr_keys["up"], 0, 8, DFF + fp * 256, 256)
                for fl in range(2):
                    f = 2 * fp + fl
                    hk = [("hmT", j) for j in range(4)]
                    bg, bgk = projA(wgv, wgk, fl, rhs_keys=hk, rhs=hmT)
                    bu, buk = projA(wuv, wuk, fl, rhs_keys=hk, rhs=hmT)
                    sg, sgk = tmp()
                    S.act(lambda e, sg=sg, bg=bg: e.activation(out=sg[:], in_=bg[:], func=AF.Silu), w=[sgk, bgk])
                    S.dve(lambda e, sg=sg, bu=bu, f=f: e.tensor_tensor(out=bigb[:, f * 512:(f + 1) * 512], in0=bu[:], in1=sg[:], op=ALU.mult), r=[sgk], w=[("big", f), buk])

        def final_norm(st, xb):
            xt = xts[xb]
            rms_stats(xb, 2)
            rstd = rstds[2]
            for j in range(4):
                S.dve(lambda e: e.scalar_tensor_tensor(out=xt[:, j, :], in0=xt[:, j, :], scalar=rstd[:, j:j + 1], in1=Gf[:], op0=ALU.mult, op1=ALU.mult), r=[("x", xb, j), ("rstd", 2), "Gf"], w=[("x", xb, j)])
            S.dma("sp", lambda e: e.dma_start(out=out_d[st * T:(st + 1) * T, :].rearrange("(j p) d -> p j d", p=128), in_=xt[:]), r=[("x", xb, j) for j in range(4)], out=True)

        def load_x(src, st, xb):
            S.dma("sp", lambda e: e.dma_start(out=xts[xb][:], in_=src[st * T:(st + 1) * T, :].rearrange("(j p) d -> p j d", p=128)), w=[("x", xb, j) for j in range(4)])

        n_rest = (len(rest_list) + max(NP - 1, 1) - 1) // max(NP - 1, 1)
        def xsrc(g):
            return (xp, g) if g < NP else (xm, g - NP)

        def par(st):
            if (NP - 1 - st) % 2 == 0:
                return 8, vtok, vkey_main
            return 0, vtok_alt, vkey_alt

        def front_prev_pieces(st):
            kb, vdst, vkeyf = par(st)
            return [
                lambda: (cast_rest(st, n_rest, st % 2), rmsnorm_T(G1c, 0, st % 2)),
                lambda: qk_part(list(range(8, 12)), kb=kb, res=True),
                lambda: qk_part(list(range(12, 16)), kb=kb, res=True),
                lambda: (gates_part(), v_part(vdst, vkeyf)),
            ]

        def mlstm_prev_steps(kb, vsrc, vkeyf):
            ctxs = [None] * 4

            def step(j):
                if j == 0:
                    ctxs[0] = ml_A(0, False, kb)
                if j < 3:
                    ctxs[j + 1] = ml_A(j + 1, False, kb)
                ml_B(ctxs[j], False, vsrc, vkeyf)
            return [lambda j=j: step(j) for j in range(4)]

        load_x(*xsrc(0), 0)
        for hf_ in range(2):
            S.dma("sp", lambda e: e.dma_start(out=kres[:, :, hf_ * 512:(hf_ + 1) * 512], in_=s_in[:, 3072 + hf_ * 512:3072 + (hf_ + 1) * 512].rearrange("(k p) n -> p k n", p=128)), r=in_keys(3072, 4096), w=KRES_K)
        load_x(*xsrc(1), 1)
        make_diag_scratch_qk()
        mod_part1()
        for p_ in front_prev_pieces(0):
            p_()
        mod_part2()
        for st in range(NP):
            last = st == NP - 1
            kb, vsrc, vkeyf = par(st)
            gates_B()
            if st + 2 <= NP:
                load_x(*xsrc(st + 2), st % 2)
            steps = mlstm_prev_steps(kb, vsrc, vkeyf)
            if not last:
                pieces = front_prev_pieces(st + 1)
                pieces[0]()
                steps[0]()
                pieces[1]()
                steps[1]()
                pieces[2]()
                steps[2]()
                pieces[3]()
                scaled_scratch((30 + NP - 1) // NP)
                make_diag_scratch_conv((8 + NP - 1) // NP)
                steps[3]()
            else:
                scaled_scratch((30 + NP - 1) // NP)
                make_diag_scratch_conv((8 + NP - 1) // NP)
                glu_part(False)
                qk_part(list(range(0, 8)))
                for sp_ in steps:
                    sp_()
        xi = NP
        fl = C("flag")
        for h in range(4):
            S.dve(lambda e: e.tensor_scalar(out=Cst[:, h, :], in0=Cst[:, h, :], scalar1=fl, scalar2=None, op0=ALU.mult), r=[("C", h), "cols"], w=[("C", h)])
        S.dve(lambda e: e.tensor_scalar(out=nst[:], in0=nst[:], scalar1=fl, scalar2=None, op0=ALU.mult), r=["nst", "cols"], w=["nst"])
        S.dve(lambda e: e.tensor_scalar(out=mst[:], in0=mst[:], scalar1=cols[0:4, _COLS["flag"]:_COLS["flag"] + 1], scalar2=None, op0=ALU.mult), r=["mst", "cols"], w=["mst"])
        S.dve(lambda e: e.tensor_scalar(out=halo[:], in0=halo[:], scalar1=fl, scalar2=None, op0=ALU.mult), r=["halo", "cols"], w=["halo"])
        for cc in range(8):
            S.dve(lambda e: e.tensor_scalar(out=aT[:, cc, 0:30], in0=aT[:, cc, 0:30], scalar1=fl, scalar2=None, op0=ALU.mult), r=[("aT", cc), "cols"], w=[("aT", cc)])

        def front(xb):
            rmsnorm_T(G1c, 0, xb)
            qk_part(list(range(8, 16)))
            gates_part()
            v_part()
            gates_B()

        for src_, dst_, k in up_list:
            S.dma("pool", lambda e: e.dma_start(out=dst_, in_=src_), w=[k])
        front(xi % 2)
        pending_final = None
        for st in range(NS):
            xb = xi % 2
            nxt = st + 1 < NS
            qk_part(list(range(0, 8)))
            glu_part(True)
            if pending_final is not None:
                final_norm(*pending_final)
                pending_final = None
            if nxt:
                load_x(xm, st + 1, (xi + 1) % 2)
            mlstm_all(True)
            if nxt:
                rms_stats((xi + 1) % 2, 0)
            merge_part()
            resid_proj("out", xb)
            rms_stats(xb, 1)
            ffn_stats_done[0] = True
            if nxt:
                front((xi + 1) % 2)
            rmsnorm_T(G2c, 16, xb, dst=hmT)
            ffn_up()
            resid_proj("dn", xb)
            pending_final = (st, xb)
            xi += 1
        final_norm(*pending_final)

        S.emit(block, sems, dsems)
    return nc


def _colform(v, n):
    buf = np.zeros(n * 128, np.float32)
    buf[:v.size] = np.asarray(v, np.float32).ravel()
    return np.ascontiguousarray(buf.reshape(n, 128).T)


_NC_CACHE = {}
_PREP_ONLY = [False]
_DBG = {}


def kernel(x, c, w_ada, b_ada, norm_mix_g, w_in, b_in, conv_dw_w, conv_dw_b, conv_ln_g, conv_ln_b, w_conv_pw,
           b_conv_pw, qk_conv_w, qk_conv_b, mlstm_norm_g, w_out, norm_ffn_g, w_ffn_up, w_ffn_down, final_norm_g):
    x = np.asarray(x, np.float32)
    B, SEQ, _ = x.shape
    half = SEQ // 2
    NS = half // T
    NP = NS
    f = lambda a: np.ascontiguousarray(np.asarray(a, np.float32))
    b_in0 = f(b_in)[0]
    bada = f(b_ada)[0]
    common = np.zeros((128, NCOLS), np.float32)

    def put(name, arr):
        common[:, _COLS[name]:_COLS[name] + arr.shape[1]] = arr

    put("b_glu", _colform(b_in0[0:2048], 16))
    put("b_qk", _colform(b_in0[2048:4096], 16))
    put("b_o", _colform(b_in0[5120:6144], 8))
    put("b_mc", _colform(b_in0[6152:7176], 8))
    put("b_mm", _colform(b_in0[7176:8200], 8))
    cw = f(conv_dw_w)[0]
    put("convw", np.ascontiguousarray(cw.reshape(31, 8, 128).transpose(2, 1, 0).reshape(128, 248)))
    put("convb", _colform(f(conv_dw_b)[0], 8))
    put("lng", _colform(f(conv_ln_g)[0], 8))
    put("lnb", _colform(f(conv_ln_b)[0], 8))
    put("b_pw", _colform(f(b_conv_pw)[0], 8))
    qw = f(qk_conv_w)[0]
    put("qkw", np.ascontiguousarray(qw.reshape(4, 16, 128).transpose(2, 1, 0).reshape(128, 64)))
    put("qkb", _colform(f(qk_conv_b)[0], 16))
    put("mng", _colform(f(mlstm_norm_g)[0], 8))
    put("nmg", _colform(f(norm_mix_g)[0], 8))
    put("nfg", _colform(f(norm_ffn_g)[0], 8))
    put("bada", np.concatenate([_colform(bada[0:1024], 8), _colform(bada[1024:2048], 8),
                                _colform(bada[3072:4096], 8), _colform(bada[4096:5120], 8)], axis=1))
    gi = np.zeros((128, 1), np.float32)
    gi[0:4, 0] = b_in0[6144:6148]
    gf = np.zeros((128, 1), np.float32)
    gf[0:4, 0] = b_in0[6148:6152]
    put("b_i", gi)
    put("b_f", gf)
    rows = np.ascontiguousarray(np.stack([bada[2048:3072], bada[5120:6144], f(final_norm_g), b_in0[4096:5120]]))
    shared = {"rows": rows, "w_ada": f(w_ada)[0], "w_in": f(w_in)[0], "w_pw": f(w_conv_pw)[0], "w_out": f(w_out)[0],
              "w_up": f(w_ffn_up)[0], "w_dn": f(w_ffn_down)[0]}
    in_maps = []
    cf = f(c)
    for core in range(2 * B):
        b, hf = core // 2, core % 2
        cols = common.copy()
        cols[:, _COLS["ccol"]:_COLS["ccol"] + 8] = _colform(cf[b], 8)
        cols[:, _COLS["flag"]] = float(hf)
        m = dict(shared)
        m["cols"] = cols
        m["xm"] = np.ascontiguousarray(x[b, hf * half:(hf + 1) * half])
        m["xp"] = np.ascontiguousarray(x[b, 0:half])
        in_maps.append(m)
    if _PREP_ONLY[0]:
        return in_maps, (NS, NP)
    key = (NS, NP)
    if key not in _NC_CACHE:
        _NC_CACHE[key] = build(NS, NP)
    nc = _NC_CACHE[key]
    res = run_bass_kernel_spmd(nc, in_maps, core_ids=list(range(2 * B)))
    out = np.empty((B, SEQ, D), np.float32)
    for core in range(2 * B):
        b, hf = core // 2, core % 2
        out[b, hf * half:(hf + 1) * half] = np.asarray(res.results[core]["out"], np.float32)
    return out
```
